# Optimizing a Trainium2 kernel written in Bass

```python
import math
import jax, jax.numpy as jnp
from jax import lax
import numpy as np

D_MODEL = 2048
BATCH = 8
SEQ = 2048
DEPTH = 4

HEAD_DIM = 128
A_PATTERNS = ((128, 1), (512, 4), (2048, 16))
A_GROUPS = len(A_PATTERNS)
A_HEADS_PER_GROUP = 4
A_HEADS = A_GROUPS * A_HEADS_PER_GROUP
B_HEADS = 8
N_A = 3 * A_HEADS * HEAD_DIM
N_B = 3 * B_HEADS * HEAD_DIM
N_IN = N_A + N_B + B_HEADS + 2 * D_MODEL
A_OUT = A_HEADS_PER_GROUP * HEAD_DIM
B_OUT = B_HEADS * HEAD_DIM
N_BUCKETS = 32
REL_MAX_DIST = 2048
D_FF = 5632
CONV_WIDTH = 3
Q_BLOCK = 128
EPS = 1e-6
NEG = -1e30

kernel_name = "hybrid_gated_dilated_fox_convffn"


def rms_norm(x, g):
    xf = x.astype(jnp.float32)
    y = xf * lax.rsqrt(jnp.mean(xf * xf, axis=-1, keepdims=True) + EPS)
    return (y * g.astype(jnp.float32)).astype(x.dtype)


def rel_bucket(dist):
    max_exact = N_BUCKETS // 2
    nf = jnp.maximum(dist, 1).astype(jnp.float32)
    large = max_exact + (jnp.log(nf / max_exact) / math.log(REL_MAX_DIST / max_exact)
                         * (N_BUCKETS - max_exact)).astype(jnp.int32)
    large = jnp.minimum(large, N_BUCKETS - 1)
    return jnp.where(dist < max_exact, dist, large)


def dilated_group(q, k, v, bias_table, window, dilation):
    b, t, h, hd = q.shape
    length = t // dilation
    span = window // dilation
    qb = min(Q_BLOCK, length)
    n_prev = -(-span // qb)
    nb = -(-length // qb)
    lp = nb * qb

    def to_strided(a):
        return a.reshape(b, length, dilation, h, hd).transpose(0, 2, 3, 1, 4)

    qs, ks, vs = to_strided(q), to_strided(k), to_strided(v)
    qs = jnp.pad(qs, [(0, 0)] * 3 + [(0, lp - length), (0, 0)]).reshape(b, dilation, h, nb, qb, hd)
    pad_kv = [(0, 0)] * 3 + [(n_prev * qb, lp - length), (0, 0)]
    ks = jnp.pad(ks, pad_kv).reshape(b, dilation, h, nb + n_prev, qb, hd)
    vs = jnp.pad(vs, pad_kv).reshape(b, dilation, h, nb + n_prev, qb, hd)
    kw = jnp.concatenate([ks[:, :, :, j:j + nb] for j in range(n_prev + 1)], axis=4)
    vw = jnp.concatenate([vs[:, :, :, j:j + nb] for j in range(n_prev + 1)], axis=4)
    kwidth = (n_prev + 1) * qb

    delta = jnp.arange(qb)[:, None] + n_prev * qb - jnp.arange(kwidth)[None, :]
    key_pos = (jnp.arange(nb)[:, None] - n_prev) * qb + jnp.arange(kwidth)[None, :]
    valid = ((delta >= 0) & (delta <= span))[None] & (key_pos >= 0)[:, None, :]
    bias = bias_table[rel_bucket(jnp.maximum(delta, 0) * dilation)].transpose(2, 0, 1)

    s = jnp.einsum('brhnqd,brhnkd->brhnqk', qs, kw).astype(jnp.float32) * (HEAD_DIM ** -0.5)
    s = jnp.where(valid, s + bias[:, None].astype(jnp.float32), NEG)
    m = jnp.max(s, axis=-1, keepdims=True)
    p = jnp.exp(s - m)
    l = jnp.sum(p, axis=-1)
    o = jnp.einsum('brhnqk,brhnkd->brhnqd', p.astype(vw.dtype), vw).astype(jnp.float32) / l[..., None]

    o = o.reshape(b, dilation, h, lp, hd)[:, :, :, :length].transpose(0, 3, 1, 2, 4).reshape(b, t, h, hd)
    m = m[..., 0].reshape(b, dilation, h, lp)[..., :length].transpose(0, 3, 1, 2).reshape(b, t, h)
    l = l.reshape(b, dilation, h, lp)[..., :length].transpose(0, 3, 1, 2).reshape(b, t, h)
    return o, m, l


def dilated_mixture(a_qkv, rel_bias):
    outs, maxes, dens = [], [], []
    for gi, (window, dilation) in enumerate(A_PATTERNS):
        table = rel_bias[:, gi * A_HEADS_PER_GROUP:(gi + 1) * A_HEADS_PER_GROUP]
        o, m, l = dilated_group(a_qkv[:, :, 0, gi], a_qkv[:, :, 1, gi], a_qkv[:, :, 2, gi],
                                table, window, dilation)
        outs.append(o); maxes.append(m); dens.append(l)
    o = jnp.stack(outs)
    m = jnp.stack(maxes)
    l = jnp.stack(dens)
    w = l * jnp.exp(m - jnp.max(m, axis=0, keepdims=True))
    y = jnp.sum(w[..., None] * o, axis=0) / jnp.sum(w, axis=0)[..., None]
    b, t = y.shape[:2]
    return y.reshape(b, t, A_OUT)


def forgetting_attention(q, k, v, log_f):
    b, t, h, hd = q.shape
    nb = t // Q_BLOCK
    c = jnp.cumsum(log_f.astype(jnp.float32), axis=1).transpose(0, 2, 1)
    kt = k.transpose(0, 2, 1, 3)
    vt = v.transpose(0, 2, 1, 3)
    qblk = q.reshape(b, nb, Q_BLOCK, h, hd).transpose(1, 0, 3, 2, 4)
    cblk = c.reshape(b, h, nb, Q_BLOCK).transpose(2, 0, 1, 3)
    kpos = jnp.arange(t)

    def one_block(args):
        qi, ci, i = args
        s = jnp.einsum('bhqd,bhkd->bhqk', qi, kt).astype(jnp.float32) * (HEAD_DIM ** -0.5)
        s = s + ci[..., None] - c[:, :, None, :]
        qpos = i * Q_BLOCK + jnp.arange(Q_BLOCK)
        s = jnp.where(kpos[None, :] <= qpos[:, None], s, NEG)
        p = jax.nn.softmax(s, axis=-1)
        return jnp.einsum('bhqk,bhkd->bhqd', p.astype(vt.dtype), vt)

    o = lax.map(one_block, (qblk, cblk, jnp.arange(nb)))
    return o.transpose(1, 0, 3, 2, 4).reshape(b, t, h * hd)


def conv_ffn(h, w_up, conv_w, conv_b, w_down):
    t = h.shape[1]
    u = h @ w_up
    up = jnp.pad(u, ((0, 0), (CONV_WIDTH - 1, 0), (0, 0)))
    uc = sum(conv_w[j] * up[:, j:j + t] for j in range(CONV_WIDTH)) + conv_b
    gate, val = uc[..., :D_FF], uc[..., D_FF:]
    return (jax.nn.gelu(gate, approximate=True) * val) @ w_down


def setup_inputs(seed: int = 0) -> dict:
    key = jax.random.key(seed)
    ks = jax.random.split(key, 16)
    f32 = jnp.float32
    nrm = lambda k, shape, scale: jax.random.normal(k, shape, f32) * scale
    gain = lambda k: 1.0 + 0.05 * jax.random.normal(k, (DEPTH, D_MODEL), f32)
    return {
        "x": jax.random.normal(ks[0], (BATCH, SEQ, D_MODEL), f32),
        "rel_bias": nrm(ks[1], (N_BUCKETS, A_HEADS), 0.5),
        "w_in": nrm(ks[2], (DEPTH, D_MODEL, N_IN), D_MODEL ** -0.5),
        "b_f": 3.0 + 0.5 * jax.random.normal(ks[3], (DEPTH, B_HEADS), f32),
        "w_pa": nrm(ks[4], (DEPTH, A_OUT, D_MODEL), A_OUT ** -0.5),
        "w_pb": nrm(ks[5], (DEPTH, B_OUT, D_MODEL), B_OUT ** -0.5),
        "w_o": nrm(ks[6], (DEPTH, D_MODEL, D_MODEL), D_MODEL ** -0.5),
        "w_up": nrm(ks[7], (DEPTH, D_MODEL, 2 * D_FF), D_MODEL ** -0.5),
        "conv_w": nrm(ks[8], (DEPTH, CONV_WIDTH, 2 * D_FF), CONV_WIDTH ** -0.5),
        "conv_b": nrm(ks[9], (DEPTH, 2 * D_FF), 0.02),
        "w_down": nrm(ks[10], (DEPTH, D_FF, D_MODEL), D_FF ** -0.5),
        "g_mix_pre": gain(ks[11]),
        "g_mix_post": gain(ks[12]),
        "g_ffn_pre": gain(ks[13]),
        "g_ffn_post": gain(ks[14]),
    }


def reference(x, rel_bias, w_in, b_f, w_pa, w_pb, w_o, w_up, conv_w, conv_b, w_down,
              g_mix_pre, g_mix_post, g_ffn_pre, g_ffn_post):
    b, t, _ = x.shape
    for layer in range(DEPTH):
        h = rms_norm(x, g_mix_pre[layer])
        proj = h @ w_in[layer]
        a_qkv = proj[..., :N_A].reshape(b, t, 3, A_GROUPS, A_HEADS_PER_GROUP, HEAD_DIM)
        off = N_A
        b_qkv = proj[..., off:off + N_B].reshape(b, t, 3, B_HEADS, HEAD_DIM)
        off += N_B
        f_logit = proj[..., off:off + B_HEADS]
        off += B_HEADS
        gates = jax.nn.sigmoid(proj[..., off:].astype(jnp.float32)).reshape(b, t, 2, D_MODEL)

        y_a = dilated_mixture(a_qkv, rel_bias).astype(x.dtype)
        log_f = jax.nn.log_sigmoid((f_logit + b_f[layer]).astype(jnp.float32))
        y_b = forgetting_attention(b_qkv[:, :, 0], b_qkv[:, :, 1], b_qkv[:, :, 2], log_f)

        merged = (gates[:, :, 0] * (y_a @ w_pa[layer]).astype(jnp.float32)
                  + gates[:, :, 1] * (y_b @ w_pb[layer]).astype(jnp.float32)).astype(x.dtype)
        x = x + rms_norm(merged @ w_o[layer], g_mix_post[layer])

        h = rms_norm(x, g_ffn_pre[layer])
        y = conv_ffn(h, w_up[layer], conv_w[layer], conv_b[layer], w_down[layer])
        x = x + rms_norm(y, g_ffn_post[layer])
    return x
```

```python
import math
from contextlib import ExitStack

import numpy as np
import concourse.bass as bass
import concourse.mybir as mybir
from concourse.bass_utils import run_bass_kernel_spmd

F32 = mybir.dt.float32
BF16 = mybir.dt.bfloat16
AF = mybir.ActivationFunctionType
ALU = mybir.AluOpType

D = 2048
T = 2048
DEPTH = 4
NKC = 16
NFC = 44
EPS = 1e-6
SCALE = 128 ** -0.5
NEGM = -30000.0
DIL = (1, 4, 16)
NSLOT = 5
SMW = 544
N_CORES = 8

STREAMS = ("pe", "act", "dve", "pool", "sp")
DMA_POOL = 8


def _ssl(start, cnt, step):
    return slice(start, start + step * (cnt - 1) + 1, step)


class Ev:
    __slots__ = ("stream", "seq", "sem", "val", "need", "count")

    def __init__(self, stream=None, seq=None, sem=None, val=None):
        self.stream = stream
        self.seq = seq
        self.sem = sem
        self.val = val
        self.need = False
        self.count = None


class Tile:
    __slots__ = ("name", "w", "r", "arena", "lo", "hi", "excl", "multi")

    def __init__(self, name, arena=None, lo=0, hi=0, excl=False, multi=False):
        self.name = name
        self.w = []
        self.r = {}
        self.arena = arena
        self.lo = lo
        self.hi = hi
        self.excl = excl
        self.multi = multi


class Op:
    __slots__ = ("fn", "waits", "ev", "dma")

    def __init__(self, fn, waits, ev, dma):
        self.fn = fn
        self.waits = waits
        self.ev = ev
        self.dma = dma


class Prog:
    def __init__(self):
        self.ops = {s: [] for s in STREAMS}
        self.waited_seq = {s: {t: -1 for t in STREAMS} for s in STREAMS}
        self.waited_dma = {s: {} for s in STREAMS}
        self.dma_count = {s: 0 for s in STREAMS}
        self.arenas = {}
        self.final_evs = []
        self.phase = ""
        self.names = None
        self.labels = {s: [] for s in STREAMS}

    def tile(self, name, arena=None, lo=0, hi=0, excl=False, multi=False):
        t = Tile(name, arena, lo, hi, excl, multi)
        if arena is not None:
            self.arenas.setdefault(arena, []).append(t)
        return t

    def _deps(self, stream, reads, writes):
        deps = []
        for t in reads:
            if t.excl:
                for k, e in t.r.items():
                    if k != stream:
                        deps.append(e)
                continue
            deps.extend(t.w)
        for t in writes:
            if t.excl:
                for k, e in t.r.items():
                    if k != stream:
                        deps.append(e)
                continue
            if not t.multi:
                for e in t.w:
                    if e.stream is None or e.stream != stream:
                        deps.append(e)
            for k, e in t.r.items():
                if k != stream:
                    deps.append(e)
            if t.arena is not None:
                for o in self.arenas[t.arena]:
                    if o is not t and o.lo < t.hi and t.lo < o.hi:
                        deps.extend(o.w)
                        deps.extend(o.r.values())
        return deps

    def _filter(self, stream, deps):
        out = []
        ws = self.waited_seq[stream]
        wd = self.waited_dma[stream]
        for e in deps:
            if e.stream is not None:
                if e.stream == "pe" and stream == "pe":
                    continue
                if e.seq > ws[e.stream]:
                    ws[e.stream] = e.seq
                    out.append(e)
            else:
                if e.val > wd.get(e.sem, 0):
                    wd[e.sem] = e.val
                    out.append(e)
        return out

    def _update(self, key, ev, reads, writes):
        for t in reads:
            if t.excl:
                t.r = {key: ev}
            else:
                t.r[key] = ev
        for t in writes:
            if t.excl:
                t.r = {key: ev}
            elif t.multi and not t.r:
                t.w.append(ev)
            else:
                t.w = [ev]
                t.r = {}

    def op(self, stream, fn, reads=(), writes=()):
        deps = self._filter(stream, self._deps(stream, reads, writes))
        ev = Ev(stream=stream, seq=len(self.ops[stream]))
        for e in deps:
            e.need = True
        self.ops[stream].append(Op(fn, deps, ev, None))
        self.labels[stream].append(self.phase)
        self._update(stream, ev, reads, writes)
        return ev

    def dma(self, stream, fn, reads=(), writes=(), final=False):
        k = self.dma_count[stream]
        self.dma_count[stream] = k + 1
        semid = (stream, k % DMA_POOL)
        val = 16 * (k // DMA_POOL + 1)
        deps = self._deps(stream + "_dma", reads, writes)
        if k >= DMA_POOL:
            deps.append(Ev(sem=semid, val=val - 16))
        deps = self._filter(stream, deps)
        for e in deps:
            e.need = True
        ev = Ev(sem=semid, val=val)
        self.ops[stream].append(Op(fn, deps, None, ev))
        self.labels[stream].append(self.phase)
        self._update(stream + "_dma%d" % k, ev, reads, writes)
        if final:
            self.final_evs.append(ev)
        return ev

    def emit(self, nc):
        for s in STREAMS:
            c = 0
            for o in self.ops[s]:
                if o.ev is not None and o.ev.need:
                    c += 1
                    o.ev.count = c
        fin = self._filter("sp", list(self.final_evs))
        with ExitStack() as es:
            sems = {s: es.enter_context(nc.semaphore("s_" + s)) for s in STREAMS}
            dsems = {}
            for s in STREAMS:
                if self.dma_count[s]:
                    for i in range(DMA_POOL):
                        dsems[(s, i)] = es.enter_context(nc.semaphore("d_%s%d" % (s, i)))
            block = es.enter_context(nc.Block())
            ops = self.ops
            names = self.names

            def run(stream, eng):
                for o in ops[stream]:
                    for e in o.waits:
                        if e.stream is not None:
                            eng.wait_ge(sems[e.stream], e.count)
                        else:
                            eng.wait_ge(dsems[e.sem], e.val)
                    ins = o.fn(eng)
                    if names is not None:
                        names[stream].append(ins.ins.name)
                    if o.dma is not None:
                        ins.then_inc(dsems[o.dma.sem], 16)
                    elif o.ev.need:
                        ins.then_inc(sems[stream], 1)
                if stream == "sp":
                    for e in fin:
                        eng.wait_ge(dsems[e.sem], e.val)

            @block.tensor
            def _(e):
                run("pe", e)

            @block.scalar
            def _(e):
                run("act", e)

            @block.vector
            def _(e):
                run("dve", e)

            @block.gpsimd
            def _(e):
                run("pool", e)

            @block.sync
            def _(e):
                run("sp", e)


class Builder:
    def __init__(self, L):
        self.L = L
        self.nc = nc = bass.Bass("TRN2", target_bir_lowering=False)
        self.P = Prog()
        dt = nc.dram_tensor
        self.xT = dt("xT", [8, 128, 4096], F32, kind="ExternalInput").ap()
        self.w16 = dt("w16", [L * 196, 128, 2048], F32, kind="ExternalInput").ap()
        self.wpa = dt("wpa", [L * 16, 128, 512], F32, kind="ExternalInput").ap()
        self.wpb = dt("wpb", [L * 16, 128, 1024], F32, kind="ExternalInput").ap()
        self.wdn = dt("wdn", [L * 16, 128, 5632], F32, kind="ExternalInput").ap()
        self.wfd = dt("wf", [L, 128, 128], F32, kind="ExternalInput").ap()
        self.smd = dt("sm", [L, 128, SMW], F32, kind="ExternalInput").ap()
        self.bmd = dt("bm", [12, 128, 256], F32, kind="ExternalInput").ap()
        self.cstd = dt("cst", [128, 384], F32, kind="ExternalInput").ap()
        self.cstbd = dt("cstb", [128, 384], F32, kind="ExternalInput").ap()
        self.outT = dt("outT", [8, 128, 4096], F32, kind="ExternalOutput").ap()
        self.xs = dt("xs", [8, 128, 4096], F32, kind="Internal").ap()
        self.mixa = dt("mixa", [8, 128, 4096], F32, kind="Internal").ap()
        self.mixf = dt("mixf", [8, 128, 4096], F32, kind="Internal").ap()
        self.mrg = dt("mrg", [NKC, 128, T], BF16, kind="Internal").ap()

    def alloc(self, es):
        nc, P = self.nc, self.P
        sb = lambda name, shape, dtp: es.enter_context(nc.sbuf_tensor(name, shape, dtp))
        self.HT = sb("HT", [128, 32768], BF16)
        self.HTf = self.HT.bitcast(F32)
        self.BG = sb("BG", [128, 45056], BF16)
        self.BGf = self.BG.bitcast(F32)
        self.wr = [sb("wr%d" % i, [128, 2048], BF16) for i in range(NSLOT)]
        self.wr_t = [P.tile("wr%d" % i) for i in range(NSLOT)]
        self.wcnt = 0
        self.cstf = sb("cstf", [128, 384], F32)
        self.cstb = sb("cstb_s", [128, 384], BF16)
        self.epsc = sb("epsc", [128, 2], F32)
        self.sm = [sb("sm%d" % i, [128, SMW], F32) for i in range(2)]
        self.sm_t = [P.tile("sm%d" % i) for i in range(2)]
        self.wf = sb("wf_s", [128, 128], BF16)
        self.wf_t = P.tile("wf")
        self.halo = sb("halo", [128, 176], F32)
        self.halo_t = P.tile("halo")
        self.rs = sb("rs", [128, 2048], F32)
        self.rs_t = [P.tile("rs0"), P.tile("rs1")]
        self.rsn = [sb("rsn%d" % i, [128, 256], F32) for i in range(2)]
        self.rsn_t = [P.tile("rsn%d" % i) for i in range(2)]
        self.MS = sb("MS", [128, 8704], BF16)
        self.MSf = self.MS.bitcast(F32)
        msf = lambda lo, n: self.MSf[:, lo // 4: lo // 4 + n]
        msb = lambda lo, n: self.MS[:, lo // 2: lo // 2 + n]
        mst = lambda nm, lo, sz: P.tile(nm, "MS", lo, lo + sz)
        self.bm = [msf(0, 256), msf(1024, 256)]
        self.bm_t = [mst("bm0", 0, 1024), mst("bm1", 1024, 1024)]
        self.ts = [msf(2048, 256), msf(3072, 256)]
        self.ts_t = [mst("ts0", 2048, 1024), mst("ts1", 3072, 1024)]
        self.pt = [msb(4096, 512), msb(5120, 512)]
        self.pt_t = [mst("pt0", 4096, 1024), mst("pt1", 5120, 1024)]
        self.fb = [msf(6144, 256), msf(7168, 256)]
        self.fb_t = [mst("fb0", 6144, 1024), mst("fb1", 7168, 1024)]
        self.rec = msf(8192, 512)
        self.rec_t = mst("rec", 8192, 2048)
        self.lz, self.l1, self.tot, self.pre, self.call, self.ref = [msf(10240 + i * 512, 128) for i in range(6)]
        self.lz_t, self.l1_t, self.tot_t, self.pre_t, self.call_t, self.ref_t = [
            mst(n, 10240 + i * 512, 512) for i, n in enumerate(("lz", "l1", "tot", "pre", "call", "ref"))]
        self.osb = msf(13312, 512)
        self.osb_t = mst("osb", 13312, 2048)
        self.fs = {}
        off = 0
        for nm, sz, n in (("ugg", 4128, 1026), ("ugv", 4128, 1026), ("A", 4096, 1024), ("B", 4096, 1024)):
            self.fs[nm] = (msf(off, n), mst("fs_" + nm, off, sz))
            off += sz
        off = 0
        for nm, sz, n, isf in (("mx0", 2048, 512, True), ("mx1", 2048, 512, True),
                               ("sq0", 1024, 512, False), ("sq1", 1024, 512, False)):
            self.fs[nm] = ((msf(off, n) if isf else msb(off, n)), mst("fs_" + nm, off, sz))
            off += sz
        self.cst_t = P.tile("cst")
        self.bank = [es.enter_context(nc.psum_tensor("bk%d" % i, [128, 512], F32)) for i in range(8)]
        self.bankb = [b.bitcast(BF16) for b in self.bank]
        self.bank_t = [P.tile("bk%d" % i, excl=True) for i in range(8)]
        self.pslot = 0
        self.pending = []
        self.xs_t = [P.tile("xs%d" % i) for i in range(8)]
        self.mixa_t = [P.tile("mixa%d" % i, multi=True) for i in range(2)]
        self.mixf_t = [P.tile("mixf%d" % i, multi=True) for i in range(2)]
        self.mrg_t = P.tile("mrg", multi=True)
        self.out_t = [P.tile("out%d" % i) for i in range(8)]
        self.ht_t = [[P.tile("ht%d_%d" % (h, k), "HT", (h * 16 + k) * 2048, (h * 16 + k + 1) * 2048)
                      for k in range(16)] for h in range(2)]
        bt = lambda nm, lo, sz: (lo, P.tile(nm, "BG", lo, lo + sz))
        self.ya = [bt("ya%d" % i, i * 4096, 4096) for i in range(4)]
        self.yb = [bt("yb%d" % i, 16384 + i * 4096, 4096) for i in range(8)]
        self.qts = [bt("qt0", 49152, 4096), bt("qt1", 61440, 4096)]
        self.kts = [bt("kt0", 53248, 4096), bt("kt1", 65536, 4096)]
        self.vs = [bt("v0", 57344, 4096), bt("v1", 69632, 4096)]
        self.vt = bt("vt", 73728, 4096)
        self.uacc = bt("uacc", 77824, 8192)
        self.bg = None
        self.side = None
        self.bg_acc = 0.0
        self.deferred = []
        self.sg = [bt("sg0", 49152, 4096), bt("sg1", 53248, 4096)]
        self.m1 = bt("m1", 57344, 4096)
        self.ms = [bt("ms0", 61440, 4096), bt("ms1", 65536, 4096)]
        self.mg = [[bt("mg%d_%d" % (h, i), (h * 16 + i) * 2048, 2048) for i in range(16)] for h in range(2)]
        self.mx = [bt("mx0", 65536, 4096), bt("mx1", 69632, 4096)]
        self.sqw = [bt("sqw0", 73728, 2048), bt("sqw1", 75776, 2048)]
        self.xtb = [bt("xt0", 0, 16384), bt("xt1", 32768, 16384)]
        self.mtb = [bt("mt0", 16384, 16384), bt("mt1", 49152, 16384)]
        self.sqb = [bt("sq0", 65536, 8192), bt("sq1", 73728, 8192)]
        self.actt = [bt("actt%d" % i, i * 2048, 2048) for i in range(NFC)]

    def bgb(self, lo, n):
        return self.BG[:, lo // 2: lo // 2 + n]

    def bgf(self, lo, n):
        return self.BGf[:, lo // 4: lo // 4 + n]

    def htb(self, lo, n):
        return self.HT[:, lo // 2: lo // 2 + n]

    def htf(self, lo, n):
        return self.HTf[:, lo // 4: lo // 4 + n]

    def ht(self, hf, kc):
        return self.htb((hf * 16 + kc) * 2048, 1024)

    def mm(self, out, lhsT, rhs, start, stop, reads, writes):
        self.P.op("pe", lambda e: e.matmul(out, lhsT=lhsT, rhs=rhs, start=start, stop=stop),
                  reads, writes)

    def tr(self, out, in_, ident, reads, writes):
        self.P.op("pe", lambda e: e.transpose(out=out, in_=in_, identity=ident), reads, writes)

    def act(self, out, in_, func, reads, writes, bias=None, scale=None):
        kw = {}
        if bias is not None:
            kw["bias"] = bias
        if scale is not None:
            kw["scale"] = scale
        self.P.op("act", lambda e: e.activation(out=out, in_=in_, func=func, **kw), reads, writes)

    def copy(self, eng, out, in_, reads, writes):
        if eng == "act":
            self.P.op("act", lambda e: e.activation(out=out, in_=in_, func=AF.Copy), reads, writes)
        else:
            self.P.op(eng, lambda e: e.tensor_copy(out=out, in_=in_), reads, writes)

    def tt(self, eng, out, in0, in1, op, reads, writes):
        self.P.op(eng, lambda e: e.tensor_tensor(out=out, in0=in0, in1=in1, op=op), reads, writes)

    def tsc(self, eng, out, in0, s1, s2, op0, op1, reads, writes):
        if s2 is None:
            self.P.op(eng, lambda e: e.tensor_scalar(out=out, in0=in0, scalar1=s1, scalar2=None, op0=op0),
                      reads, writes)
        else:
            self.P.op(eng, lambda e: e.tensor_scalar(out=out, in0=in0, scalar1=s1, scalar2=s2,
                                                     op0=op0, op1=op1), reads, writes)

    def stt(self, eng, out, in0, scalar, in1, op0, op1, reads, writes):
        self.P.op(eng, lambda e: e.scalar_tensor_tensor(out=out, in0=in0, scalar=scalar, in1=in1,
                                                        op0=op0, op1=op1), reads, writes)

    def recip(self, out, in_, reads, writes):
        self.P.op("dve", lambda e: e.reciprocal(out=out, in_=in_), reads, writes)

    def memset(self, eng, ap, val, writes):
        self.P.op(eng, lambda e: e.memset(ap, val), (), writes)

    def load(self, q, out, in_, reads, writes):
        return self.P.dma(q, lambda e: e.dma_start(out=out, in_=in_), reads, writes)

    def store(self, out, in_, reads, writes, final=False):
        return self.P.dma("sp", lambda e: e.dma_start(out=out, in_=in_), reads, writes, final=final)

    def flush(self):
        pend, self.pending = self.pending, []
        for f in pend:
            f()

    def wload(self, src, width):
        i = self.wcnt % NSLOT
        self.wcnt += 1
        slot, tile = self.wr[i], self.wr_t[i]
        self.load("pool", slot[:, 0:width], src, (), [tile])
        return slot, tile

    def next_slot(self):
        s = self.pslot
        self.pslot ^= 1
        return (2 * s, 2 * s + 1)

    def proj_half(self, ws, nk, rhs):
        b = self.next_slot()
        for kc in range(nk):
            slot, wt = ws[kc // 16]
            lhsT = slot[:, (kc % 16) * 128:(kc % 16 + 1) * 128]
            for tg in range(2):
                ap, rt = rhs(kc, tg)
                self.mm(self.bank[b[tg]][:, 0:512], lhsT, ap, kc == 0, kc == nk - 1,
                        [wt, rt], [self.bank_t[b[tg]]])
        self.flush()
        return b

    def setup(self):
        self.load("sp", self.cstf[:], self.cstd, (), [self.cst_t])
        self.load("pool", self.cstb[:], self.cstbd, (), [self.cst_t])
        self.memset("dve", self.epsc[:, 0:1], EPS, [self.cst_t])
        self.memset("dve", self.epsc[:, 1:2], 1.0, [self.cst_t])
        self.ident = self.cstb[:, 0:128]
        self.ones = self.cstb[:, 128:256]
        self.trim = self.cstb[:, 256:384]
        self.negtri = self.cstf[:, 0:128]
        self.negones = self.cstf[:, 128:256]
        self.neghalf = self.cstf[:, 256:384]

    def load_smalls(self, l):
        self.load("sp", self.sm[l % 2][:], self.smd[l], (), [self.sm_t[l % 2]])

    def gain(self, l, which, kc):
        return self.sm[l % 2][:, which * 16 + kc: which * 16 + kc + 1]

    def norm_pass(self, tts, src, src_tiles, mix, mix_tiles, lpost, wpost, dst, dst_tiles,
                  lnext, wnext, final=False, pre=None):
        self.P.phase = "norm"
        P = self.P

        def issue_loads(i):
            tt = tts[i]
            b = i % 2
            xlo, xtile = self.xtb[b]
            xv = self.bgf(xlo, 4096).rearrange("p (c t) -> p c t", c=16)
            rd = [src_tiles[tt]] if src_tiles is not None else []
            self.load("sp", self.bgf(xlo, 4096), src[tt], rd, [xtile])
            if mix is not None:
                mlo, mtile = self.mtb[b]
                mv = self.bgf(mlo, 4096).rearrange("p (c t) -> p c t", c=16)
                self.load("sp", self.bgf(mlo, 4096), mix[tt], [mix_tiles[tt // 4]], [mtile])

        def residual(i):
            tt = tts[i]
            b = i % 2
            hf = tt // 4
            xlo, xtile = self.xtb[b]
            mlo, mtile = self.mtb[b]
            smt = self.sm_t[lpost % 2]
            for kc in range(16):
                mk = self.bgf(mlo + kc * 1024, 256)
                self.stt("dve", mk, mk, self.gain(lpost, wpost, kc), self.rs[:, tt * 256:(tt + 1) * 256],
                         ALU.mult, ALU.mult, [mtile, smt, self.rs_t[hf]], [mtile])
            self.tt("dve", self.bgf(xlo, 4096), self.bgf(xlo, 4096), self.bgf(mlo, 4096), ALU.add,
                    [xtile, mtile], [xtile])

        if pre == "issue":
            issue_loads(0)
            residual(0)
            return
        if pre is None:
            issue_loads(0)
        for i, tt in enumerate(tts):
            b = i % 2
            hf = tt // 4
            col = (tt % 4) * 256
            xlo, xtile = self.xtb[b]
            mlo, mtile = self.mtb[b]
            slo, stile = self.sqb[b]
            if i + 1 < len(tts):
                issue_loads(i + 1)
            if mix is not None and not (i == 0 and pre == "done"):
                smt = self.sm_t[lpost % 2]
                for kc in range(16):
                    mk = self.bgf(mlo + kc * 1024, 256)
                    self.stt("dve", mk, mk, self.gain(lpost, wpost, kc), self.rs[:, tt * 256:(tt + 1) * 256],
                             ALU.mult, ALU.mult, [mtile, smt, self.rs_t[hf]], [mtile])
                self.tt("dve", self.bgf(xlo, 4096), self.bgf(xlo, 4096), self.bgf(mlo, 4096), ALU.add,
                        [xtile, mtile], [xtile])
            if dst is not None:
                xv = self.bgf(xlo, 4096).rearrange("p (c t) -> p c t", c=16)
                self.store(dst[tt], self.bgf(xlo, 4096), [xtile], [dst_tiles[tt]], final=final)
            if lnext is not None:
                smt = self.sm_t[lnext % 2]
                self.act(self.bgb(slo, 4096), self.bgf(xlo, 4096), AF.Square, [xtile], [stile])
                bk = 4 + b
                for kc in range(16):
                    self.mm(self.bank[bk][:, 0:256], self.ones, self.bgb(slo + kc * 512, 256),
                            kc == 0, kc == 15, [stile, self.cst_t], [self.bank_t[bk]])
                self.act(self.rsn[b][:], self.bank[bk][:, 0:256], AF.Sqrt, [self.bank_t[bk], self.cst_t],
                         [self.rsn_t[b]], bias=self.epsc[:, 0:1], scale=1.0 / D)
                self.recip(self.rsn[b][:], self.rsn[b][:], [self.rsn_t[b]], [self.rsn_t[b]])
                for kc in range(16):
                    self.stt("dve", self.ht(hf, kc)[:, col:col + 256], self.bgf(xlo + kc * 1024, 256),
                             self.gain(lnext, wnext, kc), self.rsn[b][:], ALU.mult, ALU.mult,
                             [xtile, smt, self.rsn_t[b]], [self.ht_t[hf][kc]])

    def rhs_ht(self, half):
        def f(kc, tg):
            return self.ht(half, kc)[:, tg * 512:(tg + 1) * 512], self.ht_t[half][kc]
        return f

    def defer(self, fn, delay=2):
        self.deferred.append([delay, fn])

    def tick(self):
        keep = []
        for it in self.deferred:
            it[0] -= 1
            if it[0] <= 0:
                it[1]()
            else:
                keep.append(it)
        self.deferred = keep

    def flush_deferred(self):
        d, self.deferred = self.deferred, []
        for it in d:
            it[1]()

    def bg_rate(self, x):
        self.bg_acc += x
        k = int(self.bg_acc)
        self.bg_acc -= k
        if k:
            self.bg_step(k)

    def bg_step(self, k):
        if self.bg is None:
            return
        ph = self.P.phase
        self.P.phase = "proj"
        for _ in range(k):
            try:
                next(self.bg)
            except StopIteration:
                self.bg = None
                self.flush_deferred()
                break
            self.tick()
        self.P.phase = ph

    def bg_drain(self):
        while self.bg is not None:
            self.bg_step(64)
        self.flush_deferred()

    def gen_proj(self, l, unit, dst, evac_engs):
        dlo, dtile = dst
        w = self.wload(self.w16[l * 196 + unit], 2048)
        slot, wt = w
        for half in range(2):
            b = self.next_slot()
            for kc in range(16):
                lhsT = slot[:, kc * 128:(kc + 1) * 128]
                for tg in range(2):
                    self.mm(self.bank[b[tg]][:, 0:512], lhsT,
                            self.ht(half, kc)[:, tg * 512:(tg + 1) * 512], kc == 0, kc == 15,
                            [wt, self.ht_t[half][kc]], [self.bank_t[b[tg]]])
                if kc % 2 == 1:
                    yield

            def evac(b=b, half=half):
                for tg in range(2):
                    o = self.bgb(dlo + (half * 1024 + tg * 512) * 2, 512)
                    self.copy(evac_engs[tg], o, self.bank[b[tg]][:, 0:512], [self.bank_t[b[tg]]], [dtile])
            self.defer(evac)

    def gen_head(self, l, head, s):
        self.P.phase = "proj"
        if head[0] == "A":
            _, g, hh = head
            units = [(sx * 3 + g) * 4 + hh for sx in range(3)]
            d = DIL[g]
        else:
            _, h = head
            units = [36 + sx * 8 + h for sx in range(3)]
            d = 1
        engs = ("act", "dve") if head[0] == "A" else ("dve", "dve")
        for unit, dst in zip(units, (self.qts[s], self.kts[s], self.vt)):
            for _ in self.gen_proj(l, unit, dst, engs):
                self.P.phase = "proj"
                yield
        self.flush_deferred()
        vtlo, vttile = self.vt
        vlo, vtile = self.vs[s]
        nb = 16 // d
        for grp in range(2):
            b = self.next_slot()
            bk = b[0]
            for j in range(8):
                t = grp * 8 + j
                r, n = t // nb, t % nb
                s0 = r + d * 128 * n
                src = self.BG[:, _ssl(vtlo // 2 + s0, 128, d)]
                self.tr(self.bankb[bk][:, j * 128:(j + 1) * 128], src, self.ident,
                        [vttile, self.cst_t], [self.bank_t[bk]])
                if j % 4 == 3:
                    yield

            def evac(bk=bk, grp=grp):
                self.copy("dve", self.bgb(vlo + grp * 2048, 1024), self.bankb[bk][:, 0:1024],
                          [self.bank_t[bk]], [vtile])
            self.defer(evac, 1)

    def dilated(self, l, g, hh, s):
        self.P.phase = "dil%d" % g
        d = DIL[g]
        nb = 16 // d
        a = g * 4 + hh
        bmi = (hh * 3 + g) % 2
        qlo, qtile = self.qts[s]
        klo, ktile = self.kts[s]
        vlo, vtile = self.vs[s]
        ulo, utile = self.uacc
        ltiles = list(self.rs_t)
        OB, LB = 6, 7
        items = [(r, n) for r in range(d) for n in range(nb)]

        def smm(idx):
            r, n = items[idx]
            s0 = r + d * 128 * n
            nq = 256 if n + 1 < nb else 128
            sbk = 4 + idx % 2
            kap = self.BG[:, _ssl(klo // 2 + s0, 128, d)]
            qap = self.BG[:, _ssl(qlo // 2 + s0, nq, d)]
            self.mm(self.bank[sbk][:, 0:nq], kap, qap, True, True, [ktile, qtile], [self.bank_t[sbk]])

        smm(0)
        for idx, (r, n) in enumerate(items):
            t = r * nb + n
            nq = 256 if n + 1 < nb else 128
            pb = idx % 2
            sbk = 4 + pb
            if idx + 1 < len(items):
                smm(idx + 1)
            self.stt("dve", self.ts[pb][:, 0:nq], self.bank[sbk][:, 0:nq], SCALE, self.bm[bmi][:, 0:nq],
                     ALU.mult, ALU.add, [self.bank_t[sbk], self.bm_t[bmi]], [self.ts_t[pb]])
            self.act(self.pt[pb][:, 0:nq], self.ts[pb][:, 0:nq], AF.Exp, [self.ts_t[pb]], [self.pt_t[pb]])
            if 1 <= idx <= 4:
                self.run_late()
            self.bg_rate(3.3)
            cb = (n % 4) * 128
            vcur = self.bgb(vlo + t * 256, 128)
            for (bk, cur, prev) in ((OB, vcur, None if n == 0 else self.bgb(vlo + (t - 1) * 256, 128)),
                                    (LB, self.ones, None if n == 0 else self.ones)):
                o = self.bank[bk][:, cb:cb + 128]
                rd = [vtile, self.cst_t]
                if prev is not None:
                    self.mm(o, prev, self.pt[1 - pb][:, 128:256], True, False,
                            rd + [self.pt_t[1 - pb]], [self.bank_t[bk]])
                    self.mm(o, cur, self.pt[pb][:, 0:128], False, True,
                            rd + [self.pt_t[pb]], [self.bank_t[bk]])
                else:
                    self.mm(o, cur, self.pt[pb][:, 0:128], True, True,
                            rd + [self.pt_t[pb]], [self.bank_t[bk]])
            if n % 4 == 3 or n == nb - 1:
                n0 = n - n % 4
                c = (n % 4 + 1) * 128
                st = r + d * 128 * n0
                for (bk, dstv, tiles) in ((OB, self.BGf[:, _ssl(ulo // 4 + st, c, d)], [utile]),
                                          (LB, self.rs[:, _ssl(st, c, d)], ltiles)):
                    if g == 0:
                        self.copy("dve", dstv, self.bank[bk][:, 0:c], [self.bank_t[bk]], tiles)
                    else:
                        self.tt("dve", dstv, self.bank[bk][:, 0:c], dstv, ALU.add,
                                [self.bank_t[bk]] + tiles, tiles)

    def finish_a(self, hh, k):
        ph = self.P.phase
        self.P.phase = "dilfin"
        ulo, utile = self.uacc
        lt = [self.rs_t[k // 2]]
        ylo, ytile = self.ya[hh]
        c = slice(k * 512, (k + 1) * 512)
        self.recip(self.rs[:, c], self.rs[:, c], lt, lt)
        self.tt("dve", self.bgb(ylo + k * 1024, 512), self.bgf(ulo + k * 2048, 512), self.rs[:, c], ALU.mult,
                [utile] + lt, [ytile])
        self.P.phase = ph

    def forget(self, l):
        self.P.phase = "forget"
        smt = self.sm_t[l % 2]
        sm = self.sm[l % 2]
        self.load("pool", self.wf[:], self.wfd[l], (), [self.wf_t])
        for t in range(16):
            hf, col = t // 8, (t % 8) * 128
            for kc in range(16):
                self.mm(self.bank[4][:, t * 8:(t + 1) * 8], self.ht(hf, kc)[:, col:col + 128],
                        self.wf[:, kc * 8:(kc + 1) * 8], kc == 0, kc == 15,
                        [self.ht_t[hf][kc], self.wf_t], [self.bank_t[4]])
        self.tt("dve", self.lz[:], self.bank[4][:, 0:128], sm[:, 416:544], ALU.add,
                [self.bank_t[4], smt], [self.lz_t])
        self.act(self.l1[:], self.lz[:], AF.Exp, [self.lz_t], [self.l1_t], scale=-1.0)
        self.act(self.l1[:], self.l1[:], AF.Ln, [self.l1_t, self.cst_t], [self.l1_t], bias=self.epsc[:, 1:2])
        for t in range(16):
            self.mm(self.bank[5][:, t * 8:(t + 1) * 8], self.negones, self.l1[:, t * 8:(t + 1) * 8],
                    True, True, [self.l1_t, self.cst_t], [self.bank_t[5]])
        self.copy("dve", self.tot[:], self.bank[5][:, 0:128], [self.bank_t[5]], [self.tot_t])
        self.memset("dve", self.pre[:, 0:8], 0.0, [self.pre_t])
        for t in range(1, 16):
            self.tt("dve", self.pre[:, t * 8:(t + 1) * 8], self.pre[:, (t - 1) * 8:t * 8],
                    self.tot[:, (t - 1) * 8:t * 8], ALU.add, [self.pre_t, self.tot_t], [self.pre_t])
        for t in range(16):
            self.mm(self.bank[6][:, t * 8:(t + 1) * 8], self.negtri, self.l1[:, t * 8:(t + 1) * 8],
                    True, True, [self.l1_t, self.cst_t], [self.bank_t[6]])
        for t in range(16):
            self.mm(self.bank[7][:, t * 8:(t + 1) * 8], self.neghalf, self.l1[:, t * 8:(t + 1) * 8],
                    True, True, [self.l1_t, self.cst_t], [self.bank_t[7]])
        self.tt("dve", self.call[:], self.bank[6][:, 0:128], self.pre[:], ALU.add,
                [self.bank_t[6], self.pre_t], [self.call_t])
        self.tt("dve", self.ref[:], self.bank[7][:, 0:128], self.pre[:], ALU.add,
                [self.bank_t[7], self.pre_t], [self.ref_t])

    def fox_fb(self, h):
        fb, fbt = self.fb[h % 2], self.fb_t[h % 2]
        for i in range(16):
            self.tsc("dve", fb[:, i * 16:(i + 1) * 16], self.call[:, h:128:8], -1.0,
                     self.ref[:, i * 8 + h:i * 8 + h + 1], ALU.mult, ALU.add,
                     [self.call_t, self.ref_t], [fbt])

    def fox(self, l, h, s):
        self.P.phase = "fox"
        fbi = h % 2
        fb, fbt = self.fb[fbi], self.fb_t[fbi]
        if h == 0:
            self.fox_fb(0)
        qlo, qtile = self.qts[s]
        klo, ktile = self.kts[s]
        vlo, vtile = self.vs[s]
        ylo, ytile = self.yb[h]
        OB, LB = 6, 7
        items = [(G, j) for G in range(4) for j in range(4 * G + 4)]

        def smm(idx):
            G, j = items[idx]
            c0 = max(j - 4 * G, 0) * 128
            sbk = 4 + idx % 2
            self.mm(self.bank[sbk][:, c0:512], self.bgb(klo + j * 256, 128),
                    self.bgb(qlo + (G * 512 + c0) * 2, 512 - c0), True, True,
                    [ktile, qtile], [self.bank_t[sbk]])

        smm(0)
        for idx, (G, j) in enumerate(items):
            last = 4 * G + 3
            a = max(j - 4 * G, 0)
            c0 = a * 128
            pb = idx % 2
            sbk = 4 + pb
            if idx + 1 < len(items):
                smm(idx + 1)
            for ib in range(a, 4):
                i = 4 * G + ib
                blk = self.pt[pb][:, ib * 128:(ib + 1) * 128]
                self.act(blk, self.bank[sbk][:, ib * 128:(ib + 1) * 128], AF.Exp,
                         [self.bank_t[sbk], fbt], [self.pt_t[pb]],
                         bias=fb[:, i * 16 + j:i * 16 + j + 1], scale=SCALE)
                if i == j:
                    self.tt("dve", blk, blk, self.trim, ALU.mult, [self.pt_t[pb], self.cst_t],
                            [self.pt_t[pb]])
            if 1 <= idx <= 4:
                self.run_late()
            if idx == 6 and h + 1 < 8:
                self.fox_fb(h + 1)
            self.bg_rate(4.0 if (j == 0 and G > 0) else 1.1)
            for (bk, lt) in ((OB, self.bgb(vlo + j * 256, 128)), (LB, self.ones)):
                self.mm(self.bank[bk][:, c0:512], lt, self.pt[pb][:, c0:512], j == 0, j == last,
                        [vtile, self.cst_t, self.pt_t[pb]], [self.bank_t[bk]])
            if j == last:
                self.act(self.rec[:], self.bank[LB][:, 0:512], AF.Ln, [self.bank_t[LB]], [self.rec_t])
                self.copy("act", self.osb, self.bank[OB][:, 0:512], [self.bank_t[OB]], [self.osb_t])
                self.act(self.rec[:], self.rec[:], AF.Exp, [self.rec_t], [self.rec_t], scale=-1.0)
                self.tt("dve", self.bgb(ylo + G * 1024, 512), self.osb, self.rec[:], ALU.mult,
                        [self.osb_t, self.rec_t], [ytile])

    def run_late(self, all_=False):
        while self.late:
            f = self.late.pop(0)
            f()
            if not all_:
                break

    def load_bm(self, hd):
        if hd[0] == "A":
            _, g, hh = hd
            bmi = (hh * 3 + g) % 2
            self.load("sp", self.bm[bmi][:], self.bmd[g * 4 + hh], (), [self.bm_t[bmi]])

    def heads(self, l):
        hs = [("A", g, hh) for hh in range(4) for g in range(3)] + [("B", h) for h in range(8)]
        self.late = []
        self.load_bm(hs[0])
        self.bg = self.gen_head(l, hs[0], 0)
        self.bg_drain()
        for i, hd in enumerate(hs):
            if i + 1 < len(hs):
                self.load_bm(hs[i + 1])
            self.bg = self.gen_head(l, hs[i + 1], (i + 1) % 2) if i + 1 < len(hs) else None
            if hd[0] == "A":
                self.dilated(l, hd[1], hd[2], i % 2)
                self.bg_drain()
                if hd[1] == 2:
                    self.late = [(lambda hh=hd[2], k=k: self.finish_a(hh, k)) for k in range(4)]
            else:
                self.fox(l, hd[1], i % 2)
                self.bg_drain()
        self.run_late(True)

    def merge(self, l):
        self.P.phase = "merge"
        for n in range(16):
            wga = self.wload(self.w16[l * 196 + 60 + n], 2048)
            wpa = self.wload(self.wpa[l * 16 + n], 512)
            wgb = self.wload(self.w16[l * 196 + 76 + n], 2048)
            wpb = self.wload(self.wpb[l * 16 + n], 1024)
            mlo, mstile = self.ms[n % 2]
            m1lo, m1tile = self.m1
            for half in range(2):
                def rhs_a(kc, tg, half=half):
                    lo, tl = self.ya[kc]
                    return self.bgb(lo + (half * 1024 + tg * 512) * 2, 512), tl

                def rhs_b(kc, tg, half=half):
                    lo, tl = self.yb[kc]
                    return self.bgb(lo + (half * 1024 + tg * 512) * 2, 512), tl

                s0lo, s0t = self.sg[0]
                s1lo, s1t = self.sg[1]
                b = self.proj_half([wga], 16, self.rhs_ht(half))
                for tg in range(2):
                    self.act(self.bgf(s0lo + tg * 2048, 512), self.bank[b[tg]][:, 0:512], AF.Sigmoid,
                             [self.bank_t[b[tg]]], [s0t])
                b = self.proj_half([wpa], 4, rhs_a)
                for tg in range(2):
                    self.tt("dve", self.bgf(m1lo + tg * 2048, 512), self.bank[b[tg]][:, 0:512],
                            self.bgf(s0lo + tg * 2048, 512), ALU.mult, [self.bank_t[b[tg]], s0t], [m1tile])
                b = self.proj_half([wgb], 16, self.rhs_ht(half))
                for tg in range(2):
                    self.act(self.bgf(s1lo + tg * 2048, 512), self.bank[b[tg]][:, 0:512], AF.Sigmoid,
                             [self.bank_t[b[tg]]], [s1t])
                b = self.proj_half([wpb], 8, rhs_b)
                for tg in range(2):
                    sv = self.bgf(s1lo + tg * 2048, 512)
                    self.tt("dve", sv, self.bank[b[tg]][:, 0:512], sv, ALU.mult,
                            [self.bank_t[b[tg]], s1t], [s1t])
                    self.tt("dve", self.bgb(mlo + (half * 1024 + tg * 512) * 2, 512),
                            self.bgf(m1lo + tg * 2048, 512), sv, ALU.add, [m1tile, s1t], [mstile])
            self.store(self.mrg[n], self.bgb(mlo, 2048), [mstile], [self.mrg_t])

    def evac_mix(self, b, mx, sq, drams, dram_tile, ssq_banks, first, last_):
        for tg in range(2):
            mxv, mxt = mx[tg]
            sqv, sqt = sq[tg]
            self.copy("act", mxv, self.bank[b[tg]][:, 0:512], [self.bank_t[b[tg]]], [mxt])
            self.act(sqv, self.bank[b[tg]][:, 0:512], AF.Square, [self.bank_t[b[tg]]], [sqt])
            self.store(drams[tg], mxv.rearrange("p (a t) -> p a t", a=2), [mxt], [dram_tile])

        def ssq(sq=sq, ssq_banks=ssq_banks, first=first, last_=last_):
            for tg in range(2):
                sqv, sqt = sq[tg]
                bk = ssq_banks[tg]
                self.mm(self.bank[bk][:, 0:512], self.ones, sqv, first, last_,
                        [sqt, self.cst_t], [self.bank_t[bk]])
        self.pending.append(ssq)

    def make_rs(self, bk, q):
        hf = q // 2
        o = self.rs[:, q * 512:(q + 1) * 512]
        self.act(o, self.bank[bk][:, 0:512], AF.Sqrt, [self.bank_t[bk], self.cst_t], [self.rs_t[hf]],
                 bias=self.epsc[:, 0:1], scale=1.0 / D)
        self.recip(o, o, [self.rs_t[hf]], [self.rs_t[hf]])

    def side_step(self):
        if self.side is None:
            return
        ph = self.P.phase
        self.P.phase = "norm"
        try:
            next(self.side)
        except StopIteration:
            self.side = None
        self.P.phase = ph

    def side_drain(self):
        while self.side is not None:
            self.side_step()

    def gen_norm(self, tts, src, src_tiles, mix, mix_tiles, lpost, wpost, dst, dst_tiles,
                 lnext, wnext, final, xlo, mlo, slo, bk, wt=(10, 1, 2, 6, 1), eng="dve"):
        P = self.P
        xtile = P.tile("gx%d" % xlo, "BG", xlo, xlo + 16384)
        mtile = P.tile("gm%d" % mlo, "BG", mlo, mlo + 16384)
        stile = P.tile("gs%d" % slo, "BG", slo, slo + 8192)
        xv3 = self.bgf(xlo, 4096).rearrange("p (c t) -> p c t", c=16)
        mv3 = self.bgf(mlo, 4096).rearrange("p (c t) -> p c t", c=16)
        def ld_x(tt):
            rd = [src_tiles[tt]] if src_tiles is not None else []
            self.load("sp", self.bgf(xlo, 4096), src[tt], rd, [xtile])

        def ld_m(tt):
            self.load("sp", self.bgf(mlo, 4096), mix[tt], [mix_tiles[tt // 4]], [mtile])

        ld_x(tts[0])
        ld_m(tts[0])
        for _ in range(6):
            yield
        for ti, tt in enumerate(tts):
            hf = tt // 4
            col = (tt % 4) * 256
            nxt_tt = tts[ti + 1] if ti + 1 < len(tts) else None
            for _ in range(wt[0]):
                yield
            smt = self.sm_t[lpost % 2]
            for k0 in (0, 8):
                for kc in range(k0, k0 + 8):
                    mk = self.bgf(mlo + kc * 1024, 256)
                    self.stt("dve", mk, mk, self.gain(lpost, wpost, kc), self.rs[:, tt * 256:(tt + 1) * 256],
                             ALU.mult, ALU.mult, [mtile, smt, self.rs_t[hf]], [mtile])
                yield
            for _ in range(wt[1]):
                yield
            self.tt(eng, self.bgf(xlo, 2048), self.bgf(xlo, 2048), self.bgf(mlo, 2048), ALU.add,
                    [xtile, mtile], [xtile])
            yield
            self.tt(eng, self.bgf(xlo + 8192, 2048), self.bgf(xlo + 8192, 2048), self.bgf(mlo + 8192, 2048),
                    ALU.add, [xtile, mtile], [xtile])
            self.store(dst[tt], self.bgf(xlo, 4096), [xtile], [dst_tiles[tt]], final=final)
            if nxt_tt is not None:
                ld_m(nxt_tt)
            yield
            if lnext is None:
                if nxt_tt is not None:
                    ld_x(nxt_tt)
                continue
            for _ in range(wt[2]):
                yield
            self.tt("dve", self.bgb(slo, 2048), self.bgf(xlo, 2048), self.bgf(xlo, 2048), ALU.mult, [xtile], [stile])
            yield
            self.tt("dve", self.bgb(slo + 4096, 2048), self.bgf(xlo + 8192, 2048), self.bgf(xlo + 8192, 2048), ALU.mult, [xtile], [stile])
            for _ in range(1 + wt[3]):
                yield
            for kc in range(16):
                self.mm(self.bank[bk][:, 0:256], self.ones, self.bgb(slo + kc * 512, 256),
                        kc == 0, kc == 15, [stile, self.cst_t], [self.bank_t[bk]])
            for _ in range(1 + wt[4]):
                yield
            smt = self.sm_t[lnext % 2]
            self.act(self.rsn[0][:], self.bank[bk][:, 0:256], AF.Sqrt, [self.bank_t[bk], self.cst_t],
                     [self.rsn_t[0]], bias=self.epsc[:, 0:1], scale=1.0 / D)
            self.recip(self.rsn[0][:], self.rsn[0][:], [self.rsn_t[0]], [self.rsn_t[0]])
            for k0 in (0, 8):
                for kc in range(k0, k0 + 8):
                    self.stt("dve", self.ht(hf, kc)[:, col:col + 256], self.bgf(xlo + kc * 1024, 256),
                             self.gain(lnext, wnext, kc), self.rsn[0][:], ALU.mult, ALU.mult,
                             [xtile, smt, self.rsn_t[0]], [self.ht_t[hf][kc]])
                yield
            if nxt_tt is not None:
                ld_x(nxt_tt)

    def wo_phase(self, l, side_for_half1):
        self.P.phase = "wo"
        for half in range(2):
            for kc in range(16):
                lo, tl = self.mg[half][kc]
                self.load("sp", self.bgb(lo, 1024), self.mrg[kc][:, half * 1024:(half + 1) * 1024],
                          [self.mrg_t], [tl])
        it = 0
        for half in range(2):
            for n in range(16):
                w = self.wload(self.w16[l * 196 + 92 + n], 2048)

                def rhs(kc, tg, half=half):
                    lo, tl = self.mg[half][kc]
                    return self.bgb(lo + tg * 1024, 512), tl
                b = self.proj_half([w], 16, rhs)
                p = it % 2
                it += 1
                mxlo, mxt = self.mx[p]
                sqlo, sqt = self.sqw[p]
                mx = [(self.bgf(mxlo + tg * 2048, 512), mxt) for tg in range(2)]
                sq = [(self.bgb(sqlo + tg * 1024, 512), sqt) for tg in range(2)]
                drams = [self.mixa[half * 4 + tg * 2: half * 4 + tg * 2 + 2, :, n * 256:(n + 1) * 256]
                         .rearrange("a p t -> p a t") for tg in range(2)]
                self.evac_mix(b, mx, sq, drams, self.mixa_t[half], (4 + half * 2, 5 + half * 2),
                              n == 0, n == 15)
                if half == 1:
                    for _ in range(2):
                        self.side_step()
            self.flush()
            for tg in range(2):
                self.make_rs(4 + half * 2 + tg, half * 2 + tg)
            if half == 0:
                self.side = side_for_half1()
        self.side_drain()

    def ffn_half(self, l, hf):
        self.P.phase = "ffn_up"
        smt = self.sm_t[l % 2]
        sm = self.sm[l % 2]
        fs = self.fs
        av, at = fs["A"]
        bv, btl = fs["B"]
        for c in range(NFC):
            wg = self.wload(self.w16[l * 196 + 108 + c], 2048)
            wv = self.wload(self.w16[l * 196 + 108 + NFC + c], 2048)
            bg_ = self.proj_half([wg], 16, self.rhs_ht(hf))
            bv_ = self.proj_half([wv], 16, self.rhs_ht(hf))
            alo, atile = self.actt[c]
            kinds = ((bg_, c, "ugg", av, at), (bv_, NFC + c, "ugv", bv, btl))
            for (b, ch, ugn, xv, xt) in kinds:
                ug, ugt = fs[ugn]
                for tg in range(2):
                    self.copy("act", ug[:, 2 + tg * 512:2 + (tg + 1) * 512], self.bank[b[tg]][:, 0:512],
                              [self.bank_t[b[tg]]], [ugt])
            for (b, ch, ugn, xv, xt) in kinds:
                ug, ugt = fs[ugn]
                if hf == 0:
                    self.memset("dve", ug[:, 0:2], 0.0, [ugt])
                else:
                    self.copy("dve", ug[:, 0:2], self.halo[:, ch * 2:ch * 2 + 2], [self.halo_t], [ugt])
                cw = lambda j, ch=ch: sm[:, 64 + ch * 3 + j:64 + ch * 3 + j + 1]
                cb = sm[:, 328 + ch:328 + ch + 1]
                self.act(xv, ug[:, 2:1026], AF.Identity, [ugt, smt], [xt], bias=cb, scale=cw(2))
                if hf == 0:
                    self.copy("dve", self.halo[:, ch * 2:ch * 2 + 2], ug[:, 1024:1026], [ugt], [self.halo_t])
                self.stt("dve", xv, ug[:, 1:1025], cw(1), xv, ALU.mult, ALU.add, [ugt, smt, xt], [xt])
                self.stt("dve", xv, ug[:, 0:1024], cw(0), xv, ALU.mult, ALU.add, [ugt, smt, xt], [xt])
            self.act(av, av, AF.Gelu_apprx_tanh, [at], [at])
            self.tt("dve", self.bgb(alo, 1024), av, bv, ALU.mult, [at, btl], [atile])
            if c == 20:
                self.side_drain()
            for _ in range(7):
                self.side_step()
        self.side_drain()
        self.P.phase = "ffn_down"
        it = 0
        for n in range(16):
            ws = [self.wload(self.wdn[l * 16 + n][:, 0:2048], 2048),
                  self.wload(self.wdn[l * 16 + n][:, 2048:4096], 2048),
                  self.wload(self.wdn[l * 16 + n][:, 4096:5632], 1536)]

            def rhs(kc, tg):
                lo, tl = self.actt[kc]
                return self.bgb(lo + tg * 1024, 512), tl
            b = self.proj_half(ws, NFC, rhs)
            mx = [fs["mx%d" % tg] for tg in range(2)]
            sq = [fs["sq%d" % tg] for tg in range(2)]
            drams = [self.mixf[hf * 4 + tg * 2: hf * 4 + tg * 2 + 2, :, n * 256:(n + 1) * 256]
                     .rearrange("a p t -> p a t") for tg in range(2)]
            self.evac_mix(b, mx, sq, drams, self.mixf_t[hf], (4, 5), n == 0, n == 15)
        self.flush()
        for tg in range(2):
            self.make_rs(4 + tg, hf * 2 + tg)

    def build(self):
        L = self.L
        with ExitStack() as es:
            self.alloc(es)
            self.setup()
            self.load_smalls(0)
            self.norm_pass(list(range(8)), self.xT, None, None, None, None, None, None, None, 0, 0)
            for l in range(L):
                last = l == L - 1
                if not last:
                    self.load_smalls(l + 1)
                self.forget(l)
                self.heads(l)
                self.merge(l)
                src = self.xT if l == 0 else self.xs
                src_t = None if l == 0 else self.xs_t
                nxt = None if last else l + 1
                fdst = self.outT if last else self.xs
                fdst_t = self.out_t if last else self.xs_t

                def n0(l=l, src=src, src_t=src_t):
                    self.norm_pass([0, 1, 2, 3], src, src_t, self.mixa, self.mixa_t, l, 1,
                                   self.xs, self.xs_t, l, 2, pre="issue")
                    self.P.phase = "wo"
                    return None
                self.wo_phase(l, n0)
                self.norm_pass([0, 1, 2, 3], src, src_t, self.mixa, self.mixa_t, l, 1, self.xs, self.xs_t, l, 2,
                               pre="done")
                self.side = self.gen_norm([4, 5, 6, 7], src, src_t, self.mixa, self.mixa_t, l, 1,
                                          self.xs, self.xs_t, l, 2, False, 49152, 65536, 81920, 6)
                self.ffn_half(l, 0)
                self.side = self.gen_norm([0, 1, 2, 3], self.xs, self.xs_t, self.mixf, self.mixf_t, l, 3,
                                          fdst, fdst_t, nxt, 0, last, 49152, 65536, 81920, 6)
                self.ffn_half(l, 1)
                self.norm_pass([4, 5, 6, 7], self.xs, self.xs_t, self.mixf, self.mixf_t, l, 3,
                               fdst, fdst_t, nxt, 0, final=last)
            self.P.emit(self.nc)
        return self.nc


def _units(W):
    K, N = W.shape
    return np.ascontiguousarray(
        W.reshape(K // 128, 128, N // 128, 128).transpose(2, 1, 0, 3).reshape(N // 128, 128, K))


def _bucket(dist):
    dist = np.asarray(dist, dtype=np.int64)
    nf = np.maximum(dist, 1).astype(np.float32)
    large = 16 + (np.log(nf / np.float32(16)) / np.float32(math.log(2048 / 16))
                  * np.float32(16)).astype(np.int32)
    large = np.minimum(large, 31)
    return np.where(dist < 16, dist, large)


def _consts():
    k = np.arange(128)[:, None]
    q = np.arange(128)[None, :]
    negtri = -(k <= q).astype(np.float32)
    negones = -np.ones((128, 128), np.float32)
    neghalf = -np.broadcast_to((k < 64), (128, 128)).astype(np.float32)
    cst = np.concatenate([negtri, negones, neghalf], axis=1)
    ident = np.eye(128, dtype=np.float32)
    ones = np.ones((128, 128), np.float32)
    trim = (k <= q).astype(np.float32)
    cstb = np.concatenate([ident, ones, trim], axis=1)
    return np.ascontiguousarray(cst), np.ascontiguousarray(cstb)


def _bias_mats(rel_bias):
    j = np.arange(128)[:, None]
    i = np.arange(128)[None, :]
    out = np.full((12, 128, 256), NEGM, np.float32)
    for g in range(3):
        d = DIL[g]
        dc = i - j
        dn = 128 + i - j
        bc = _bucket(np.maximum(dc, 0) * d)
        bn = _bucket(np.clip(dn, 0, 128) * d)
        for hh in range(4):
            a = g * 4 + hh
            col = rel_bias[:, a]
            out[a, :, 0:128] = np.where(dc >= 0, col[bc], np.float32(NEGM))
            out[a, :, 128:256] = np.where(dn <= 128, col[bn], np.float32(NEGM))
    return out


def _layer_inputs(l, w_in, b_f, w_pa, w_pb, w_o, w_up, conv_w, conv_b, w_down,
                  g_mix_pre, g_mix_post, g_ffn_pre, g_ffn_post):
    wi = w_in[l]
    w16 = np.concatenate([_units(wi[:, :7680]), _units(wi[:, 7688:]), _units(w_o[l]), _units(w_up[l])], axis=0)
    wf = np.ascontiguousarray(wi[:, 7680:7688].reshape(16, 128, 8).transpose(1, 0, 2).reshape(128, 128))
    sm = np.empty((128, SMW), np.float32)
    for i, gsrc in enumerate((g_mix_pre, g_mix_post, g_ffn_pre, g_ffn_post)):
        sm[:, i * 16:(i + 1) * 16] = gsrc[l].reshape(16, 128).T
    sm[:, 64:328] = conv_w[l].reshape(3, 88, 128).transpose(2, 1, 0).reshape(128, 264)
    sm[:, 328:416] = conv_b[l].reshape(88, 128).T
    sm[:, 416:544] = np.tile(b_f[l][None, :], (128, 16))
    return dict(w16=w16, wpa=_units(w_pa[l]), wpb=_units(w_pb[l]), wdn=_units(w_down[l]), wf=wf, sm=sm)


_PROG_CACHE = {}


def _get_prog(L):
    if L not in _PROG_CACHE:
        _PROG_CACHE[L] = Builder(L).build()
    return _PROG_CACHE[L]


def _run(xT_list, layers, rel_bias, weights, core_ids):
    L = len(layers)
    per = [_layer_inputs(l, *weights) for l in layers]
    shared = {k: np.ascontiguousarray(np.concatenate([p[k] for p in per], axis=0)) if k in ("w16", "wpa", "wpb", "wdn")
              else np.ascontiguousarray(np.stack([p[k] for p in per], axis=0)) for k in per[0]}
    cst, cstb = _consts()
    shared["bm"] = _bias_mats(rel_bias)
    shared["cst"] = cst
    shared["cstb"] = cstb
    nc = _get_prog(L)
    in_maps = [dict(shared, xT=xt) for xt in xT_list]
    res = run_bass_kernel_spmd(nc, in_maps, core_ids=core_ids)
    return [r["outT"] for r in res.results]


def _x_to_dev(xb):
    return np.ascontiguousarray(xb.reshape(8, 256, NKC, 128).transpose(0, 3, 2, 1)).reshape(8, 128, 4096)


def _dev_to_x(o):
    return np.asarray(o).reshape(8, 128, NKC, 256).transpose(0, 3, 2, 1).reshape(T, D)


FUSE = 4


def kernel(x, rel_bias, w_in, b_f, w_pa, w_pb, w_o, w_up, conv_w, conv_b, w_down,
           g_mix_pre, g_mix_post, g_ffn_pre, g_ffn_post):
    f = lambda a: np.asarray(a, dtype=np.float32)
    x = f(x)
    weights = tuple(f(a) for a in (w_in, b_f, w_pa, w_pb, w_o, w_up, conv_w, conv_b, w_down,
                                   g_mix_pre, g_mix_post, g_ffn_pre, g_ffn_post))
    rel_bias = f(rel_bias)
    B = x.shape[0]
    xT = [_x_to_dev(x[b]) for b in range(B)]
    for l0 in range(0, DEPTH, FUSE):
        xT = _run(xT, list(range(l0, l0 + FUSE)), rel_bias, weights, list(range(B)))
    out = np.stack([_dev_to_x(o) for o in xT], axis=0)
    return np.ascontiguousarray(out.astype(np.float32))
```

```python
import math
from contextlib import ExitStack

import numpy as np
import concourse.bass as bass
import concourse.mybir as mybir
from concourse.bass_utils import run_bass_kernel_spmd

F32 = mybir.dt.float32
BF16 = mybir.dt.bfloat16
AF = mybir.ActivationFunctionType
ALU = mybir.AluOpType

D = 2048
T = 2048
DEPTH = 4
NKC = 16
NFC = 44
EPS = 1e-6
SCALE = 128 ** -0.5
NEGM = -30000.0
DIL = (1, 4, 16)
NSLOT = 5
SMW = 544
N_CORES = 8

STREAMS = ("pe", "act", "dve", "pool", "sp")
DMA_POOL = 8


def _ssl(start, cnt, step):
    return slice(start, start + step * (cnt - 1) + 1, step)


class Ev:
    __slots__ = ("stream", "seq", "sem", "val", "need", "count")

    def __init__(self, stream=None, seq=None, sem=None, val=None):
        self.stream = stream
        self.seq = seq
        self.sem = sem
        self.val = val
        self.need = False
        self.count = None


class Tile:
    __slots__ = ("name", "w", "r", "arena", "lo", "hi", "excl", "multi")

    def __init__(self, name, arena=None, lo=0, hi=0, excl=False, multi=False):
        self.name = name
        self.w = []
        self.r = {}
        self.arena = arena
        self.lo = lo
        self.hi = hi
        self.excl = excl
        self.multi = multi


class Op:
    __slots__ = ("fn", "waits", "ev", "dma")

    def __init__(self, fn, waits, ev, dma):
        self.fn = fn
        self.waits = waits
        self.ev = ev
        self.dma = dma


class Prog:
    def __init__(self):
        self.ops = {s: [] for s in STREAMS}
        self.waited_seq = {s: {t: -1 for t in STREAMS} for s in STREAMS}
        self.waited_dma = {s: {} for s in STREAMS}
        self.dma_count = {s: 0 for s in STREAMS}
        self.arenas = {}
        self.final_evs = []
        self.phase = ""
        self.names = None
        self.labels = {s: [] for s in STREAMS}

    def tile(self, name, arena=None, lo=0, hi=0, excl=False, multi=False):
        t = Tile(name, arena, lo, hi, excl, multi)
        if arena is not None:
            self.arenas.setdefault(arena, []).append(t)
        return t

    def _deps(self, stream, reads, writes):
        deps = []
        for t in reads:
            if t.excl:
                for k, e in t.r.items():
                    if k != stream:
                        deps.append(e)
                continue
            deps.extend(t.w)
        for t in writes:
            if t.excl:
                for k, e in t.r.items():
                    if k != stream:
                        deps.append(e)
                continue
            if not t.multi:
                for e in t.w:
                    if e.stream is None or e.stream != stream:
                        deps.append(e)
            for k, e in t.r.items():
                if k != stream:
                    deps.append(e)
            if t.arena is not None:
                for o in self.arenas[t.arena]:
                    if o is not t and o.lo < t.hi and t.lo < o.hi:
                        deps.extend(o.w)
                        deps.extend(o.r.values())
        return deps

    def _filter(self, stream, deps):
        out = []
        ws = self.waited_seq[stream]
        wd = self.waited_dma[stream]
        for e in deps:
            if e.stream is not None:
                if e.stream == "pe" and stream == "pe":
                    continue
                if e.seq > ws[e.stream]:
                    ws[e.stream] = e.seq
                    out.append(e)
            else:
                if e.val > wd.get(e.sem, 0):
                    wd[e.sem] = e.val
                    out.append(e)
        return out

    def _update(self, key, ev, reads, writes):
        for t in reads:
            if t.excl:
                t.r = {key: ev}
            else:
                t.r[key] = ev
        for t in writes:
            if t.excl:
                t.r = {key: ev}
            elif t.multi and not t.r:
                t.w.append(ev)
            else:
                t.w = [ev]
                t.r = {}

    def op(self, stream, fn, reads=(), writes=()):
        deps = self._filter(stream, self._deps(stream, reads, writes))
        ev = Ev(stream=stream, seq=len(self.ops[stream]))
        for e in deps:
            e.need = True
        self.ops[stream].append(Op(fn, deps, ev, None))
        self.labels[stream].append(self.phase)
        self._update(stream, ev, reads, writes)
        return ev

    def dma(self, stream, fn, reads=(), writes=(), final=False):
        k = self.dma_count[stream]
        self.dma_count[stream] = k + 1
        semid = (stream, k % DMA_POOL)
        val = 16 * (k // DMA_POOL + 1)
        deps = self._deps(stream + "_dma", reads, writes)
        if k >= DMA_POOL:
            deps.append(Ev(sem=semid, val=val - 16))
        deps = self._filter(stream, deps)
        for e in deps:
            e.need = True
        ev = Ev(sem=semid, val=val)
        self.ops[stream].append(Op(fn, deps, None, ev))
        self.labels[stream].append(self.phase)
        self._update(stream + "_dma%d" % k, ev, reads, writes)
        if final:
            self.final_evs.append(ev)
        return ev

    def emit(self, nc):
        for s in STREAMS:
            c = 0
            for o in self.ops[s]:
                if o.ev is not None and o.ev.need:
                    c += 1
                    o.ev.count = c
        fin = self._filter("sp", list(self.final_evs))
        with ExitStack() as es:
            sems = {s: es.enter_context(nc.semaphore("s_" + s)) for s in STREAMS}
            dsems = {}
            for s in STREAMS:
                if self.dma_count[s]:
                    for i in range(DMA_POOL):
                        dsems[(s, i)] = es.enter_context(nc.semaphore("d_%s%d" % (s, i)))
            block = es.enter_context(nc.Block())
            ops = self.ops
            names = self.names

            def run(stream, eng):
                for o in ops[stream]:
                    for e in o.waits:
                        if e.stream is not None:
                            eng.wait_ge(sems[e.stream], e.count)
                        else:
                            eng.wait_ge(dsems[e.sem], e.val)
                    ins = o.fn(eng)
                    if names is not None:
                        names[stream].append(ins.ins.name)
                    if o.dma is not None:
                        ins.then_inc(dsems[o.dma.sem], 16)
                    elif o.ev.need:
                        ins.then_inc(sems[stream], 1)
                if stream == "sp":
                    for e in fin:
                        eng.wait_ge(dsems[e.sem], e.val)

            @block.tensor
            def _(e):
                run("pe", e)

            @block.scalar
            def _(e):
                run("act", e)

            @block.vector
            def _(e):
                run("dve", e)

            @block.gpsimd
            def _(e):
                run("pool", e)

            @block.sync
            def _(e):
                run("sp", e)


class Builder:
    def __init__(self, L):
        self.L = L
        self.nc = nc = bass.Bass("TRN2", target_bir_lowering=False)
        self.P = Prog()
        dt = nc.dram_tensor
        self.xT = dt("xT", [8, 128, 4096], F32, kind="ExternalInput").ap()
        self.w16 = dt("w16", [L * 196, 128, 2048], F32, kind="ExternalInput").ap()
        self.wpa = dt("wpa", [L * 16, 128, 512], F32, kind="ExternalInput").ap()
        self.wpb = dt("wpb", [L * 16, 128, 1024], F32, kind="ExternalInput").ap()
        self.wdn = dt("wdn", [L * 16, 128, 5632], F32, kind="ExternalInput").ap()
        self.wfd = dt("wf", [L, 128, 128], F32, kind="ExternalInput").ap()
        self.smd = dt("sm", [L, 128, SMW], F32, kind="ExternalInput").ap()
        self.bmd = dt("bm", [12, 128, 256], F32, kind="ExternalInput").ap()
        self.cstd = dt("cst", [128, 384], F32, kind="ExternalInput").ap()
        self.cstbd = dt("cstb", [128, 384], F32, kind="ExternalInput").ap()
        self.outT = dt("outT", [8, 128, 4096], F32, kind="ExternalOutput").ap()
        self.xs = dt("xs", [8, 128, 4096], F32, kind="Internal").ap()
        self.mixa = dt("mixa", [8, 128, 4096], F32, kind="Internal").ap()
        self.mixf = dt("mixf", [8, 128, 4096], F32, kind="Internal").ap()
        self.mrg = dt("mrg", [NKC, 128, T], BF16, kind="Internal").ap()

    def alloc(self, es):
        nc, P = self.nc, self.P
        sb = lambda name, shape, dtp: es.enter_context(nc.sbuf_tensor(name, shape, dtp))
        self.HT = sb("HT", [128, 32768], BF16)
        self.HTf = self.HT.bitcast(F32)
        self.BG = sb("BG", [128, 45056], BF16)
        self.BGf = self.BG.bitcast(F32)
        self.wr = [sb("wr%d" % i, [128, 2048], BF16) for i in range(NSLOT)]
        self.wr_t = [P.tile("wr%d" % i) for i in range(NSLOT)]
        self.wcnt = 0
        self.cstf = sb("cstf", [128, 384], F32)
        self.cstb = sb("cstb_s", [128, 384], BF16)
        self.epsc = sb("epsc", [128, 2], F32)
        self.sm = [sb("sm%d" % i, [128, SMW], F32) for i in range(2)]
        self.sm_t = [P.tile("sm%d" % i) for i in range(2)]
        self.wf = sb("wf_s", [128, 128], BF16)
        self.wf_t = P.tile("wf")
        self.halo = sb("halo", [128, 176], F32)
        self.halo_t = P.tile("halo")
        self.rs = sb("rs", [128, 2048], F32)
        self.rs_t = [P.tile("rs0"), P.tile("rs1")]
        self.rsn = [sb("rsn%d" % i, [128, 256], F32) for i in range(2)]
        self.rsn_t = [P.tile("rsn%d" % i) for i in range(2)]
        self.MS = sb("MS", [128, 8704], BF16)
        self.MSf = self.MS.bitcast(F32)
        msf = lambda lo, n: self.MSf[:, lo // 4: lo // 4 + n]
        msb = lambda lo, n: self.MS[:, lo // 2: lo // 2 + n]
        mst = lambda nm, lo, sz: P.tile(nm, "MS", lo, lo + sz)
        self.bm = [msf(0, 256), msf(1024, 256)]
        self.bm_t = [mst("bm0", 0, 1024), mst("bm1", 1024, 1024)]
        self.ts = [msf(2048, 256), msf(3072, 256)]
        self.ts_t = [mst("ts0", 2048, 1024), mst("ts1", 3072, 1024)]
        self.pt = [msb(4096, 512), msb(5120, 512)]
        self.pt_t = [mst("pt0", 4096, 1024), mst("pt1", 5120, 1024)]
        self.fb = [msf(6144, 256), msf(7168, 256)]
        self.fb_t = [mst("fb0", 6144, 1024), mst("fb1", 7168, 1024)]
        self.rec = msf(8192, 512)
        self.rec_t = mst("rec", 8192, 2048)
        self.lz, self.l1, self.tot, self.pre, self.call, self.ref = [msf(10240 + i * 512, 128) for i in range(6)]
        self.lz_t, self.l1_t, self.tot_t, self.pre_t, self.call_t, self.ref_t = [
            mst(n, 10240 + i * 512, 512) for i, n in enumerate(("lz", "l1", "tot", "pre", "call", "ref"))]
        self.osb = msf(13312, 512)
        self.osb_t = mst("osb", 13312, 2048)
        self.fs = {}
        off = 0
        for nm, sz, n in (("ugg", 4128, 1026), ("ugv", 4128, 1026), ("A", 4096, 1024), ("B", 4096, 1024)):
            self.fs[nm] = (msf(off, n), mst("fs_" + nm, off, sz))
            off += sz
        off = 0
        for nm, sz, n, isf in (("mx0", 2048, 512, True), ("mx1", 2048, 512, True),
                               ("sq0", 1024, 512, False), ("sq1", 1024, 512, False)):
            self.fs[nm] = ((msf(off, n) if isf else msb(off, n)), mst("fs_" + nm, off, sz))
            off += sz
        self.cst_t = P.tile("cst")
        self.bank = [es.enter_context(nc.psum_tensor("bk%d" % i, [128, 512], F32)) for i in range(8)]
        self.bankb = [b.bitcast(BF16) for b in self.bank]
        self.bank_t = [P.tile("bk%d" % i, excl=True) for i in range(8)]
        self.pslot = 0
        self.pending = []
        self.xs_t = [P.tile("xs%d" % i) for i in range(8)]
        self.mixa_t = [P.tile("mixa%d" % i, multi=True) for i in range(2)]
        self.mixf_t = [P.tile("mixf%d" % i, multi=True) for i in range(2)]
        self.mrg_t = P.tile("mrg", multi=True)
        self.out_t = [P.tile("out%d" % i) for i in range(8)]
        self.ht_t = [[P.tile("ht%d_%d" % (h, k), "HT", (h * 16 + k) * 2048, (h * 16 + k + 1) * 2048)
                      for k in range(16)] for h in range(2)]
        bt = lambda nm, lo, sz: (lo, P.tile(nm, "BG", lo, lo + sz))
        self.ya = [bt("ya%d" % i, i * 4096, 4096) for i in range(4)]
        self.yb = [bt("yb%d" % i, 16384 + i * 4096, 4096) for i in range(8)]
        self.qts = [bt("qt0", 49152, 4096), bt("qt1", 61440, 4096)]
        self.kts = [bt("kt0", 53248, 4096), bt("kt1", 65536, 4096)]
        self.vs = [bt("v0", 57344, 4096), bt("v1", 69632, 4096)]
        self.vt = bt("vt", 73728, 4096)
        self.uacc = bt("uacc", 77824, 8192)
        self.bg = None
        self.side = None
        self.bg_acc = 0.0
        self.deferred = []
        self.sg = [bt("sg0", 49152, 4096), bt("sg1", 53248, 4096)]
        self.m1 = bt("m1", 57344, 4096)
        self.ms = [bt("ms0", 61440, 4096), bt("ms1", 65536, 4096)]
        self.mg = [[bt("mg%d_%d" % (h, i), (h * 16 + i) * 2048, 2048) for i in range(16)] for h in range(2)]
        self.mx = [bt("mx0", 65536, 4096), bt("mx1", 69632, 4096)]
        self.sqw = [bt("sqw0", 73728, 2048), bt("sqw1", 75776, 2048)]
        self.xtb = [bt("xt0", 0, 16384), bt("xt1", 32768, 16384)]
        self.mtb = [bt("mt0", 16384, 16384), bt("mt1", 49152, 16384)]
        self.sqb = [bt("sq0", 65536, 8192), bt("sq1", 73728, 8192)]
        self.actt = [bt("actt%d" % i, i * 2048, 2048) for i in range(NFC)]

    def bgb(self, lo, n):
        return self.BG[:, lo // 2: lo // 2 + n]

    def bgf(self, lo, n):
        return self.BGf[:, lo // 4: lo // 4 + n]

    def htb(self, lo, n):
        return self.HT[:, lo // 2: lo // 2 + n]

    def htf(self, lo, n):
        return self.HTf[:, lo // 4: lo // 4 + n]

    def ht(self, hf, kc):
        return self.htb((hf * 16 + kc) * 2048, 1024)

    def mm(self, out, lhsT, rhs, start, stop, reads, writes):
        self.P.op("pe", lambda e: e.matmul(out, lhsT=lhsT, rhs=rhs, start=start, stop=stop),
                  reads, writes)

    def tr(self, out, in_, ident, reads, writes):
        self.P.op("pe", lambda e: e.transpose(out=out, in_=in_, identity=ident), reads, writes)

    def act(self, out, in_, func, reads, writes, bias=None, scale=None):
        kw = {}
        if bias is not None:
            kw["bias"] = bias
        if scale is not None:
            kw["scale"] = scale
        self.P.op("act", lambda e: e.activation(out=out, in_=in_, func=func, **kw), reads, writes)

    def copy(self, eng, out, in_, reads, writes):
        if eng == "act":
            self.P.op("act", lambda e: e.activation(out=out, in_=in_, func=AF.Copy), reads, writes)
        else:
            self.P.op(eng, lambda e: e.tensor_copy(out=out, in_=in_), reads, writes)

    def tt(self, eng, out, in0, in1, op, reads, writes):
        self.P.op(eng, lambda e: e.tensor_tensor(out=out, in0=in0, in1=in1, op=op), reads, writes)

    def tsc(self, eng, out, in0, s1, s2, op0, op1, reads, writes):
        if s2 is None:
            self.P.op(eng, lambda e: e.tensor_scalar(out=out, in0=in0, scalar1=s1, scalar2=None, op0=op0),
                      reads, writes)
        else:
            self.P.op(eng, lambda e: e.tensor_scalar(out=out, in0=in0, scalar1=s1, scalar2=s2,
                                                     op0=op0, op1=op1), reads, writes)

    def stt(self, eng, out, in0, scalar, in1, op0, op1, reads, writes):
        self.P.op(eng, lambda e: e.scalar_tensor_tensor(out=out, in0=in0, scalar=scalar, in1=in1,
                                                        op0=op0, op1=op1), reads, writes)

    def recip(self, out, in_, reads, writes):
        self.P.op("dve", lambda e: e.reciprocal(out=out, in_=in_), reads, writes)

    def memset(self, eng, ap, val, writes):
        self.P.op(eng, lambda e: e.memset(ap, val), (), writes)

    def load(self, q, out, in_, reads, writes):
        return self.P.dma(q, lambda e: e.dma_start(out=out, in_=in_), reads, writes)

    def store(self, out, in_, reads, writes, final=False):
        return self.P.dma("sp", lambda e: e.dma_start(out=out, in_=in_), reads, writes, final=final)

    def flush(self):
        pend, self.pending = self.pending, []
        for f in pend:
            f()

    def wload(self, src, width):
        i = self.wcnt % NSLOT
        self.wcnt += 1
        slot, tile = self.wr[i], self.wr_t[i]
        self.load("pool", slot[:, 0:width], src, (), [tile])
        return slot, tile

    def next_slot(self):
        s = self.pslot
        self.pslot ^= 1
        return (2 * s, 2 * s + 1)

    def proj_half(self, ws, nk, rhs):
        b = self.next_slot()
        for kc in range(nk):
            slot, wt = ws[kc // 16]
            lhsT = slot[:, (kc % 16) * 128:(kc % 16 + 1) * 128]
            for tg in range(2):
                ap, rt = rhs(kc, tg)
                self.mm(self.bank[b[tg]][:, 0:512], lhsT, ap, kc == 0, kc == nk - 1,
                        [wt, rt], [self.bank_t[b[tg]]])
        self.flush()
        return b

    def setup(self):
        self.load("sp", self.cstf[:], self.cstd, (), [self.cst_t])
        self.load("pool", self.cstb[:], self.cstbd, (), [self.cst_t])
        self.memset("dve", self.epsc[:, 0:1], EPS, [self.cst_t])
        self.memset("dve", self.epsc[:, 1:2], 1.0, [self.cst_t])
        self.ident = self.cstb[:, 0:128]
        self.ones = self.cstb[:, 128:256]
        self.trim = self.cstb[:, 256:384]
        self.negtri = self.cstf[:, 0:128]
        self.negones = self.cstf[:, 128:256]
        self.neghalf = self.cstf[:, 256:384]

    def load_smalls(self, l):
        self.load("sp", self.sm[l % 2][:], self.smd[l], (), [self.sm_t[l % 2]])

    def gain(self, l, which, kc):
        return self.sm[l % 2][:, which * 16 + kc: which * 16 + kc + 1]

    def norm_pass(self, tts, src, src_tiles, mix, mix_tiles, lpost, wpost, dst, dst_tiles,
                  lnext, wnext, final=False, pre=None):
        self.P.phase = "norm"
        P = self.P

        def issue_loads(i):
            tt = tts[i]
            b = i % 2
            xlo, xtile = self.xtb[b]
            xv = self.bgf(xlo, 4096).rearrange("p (c t) -> p c t", c=16)
            rd = [src_tiles[tt]] if src_tiles is not None else []
            self.load("sp", self.bgf(xlo, 4096), src[tt], rd, [xtile])
            if mix is not None:
                mlo, mtile = self.mtb[b]
                mv = self.bgf(mlo, 4096).rearrange("p (c t) -> p c t", c=16)
                self.load("sp", self.bgf(mlo, 4096), mix[tt], [mix_tiles[tt // 4]], [mtile])

        def residual(i):
            tt = tts[i]
            b = i % 2
            hf = tt // 4
            xlo, xtile = self.xtb[b]
            mlo, mtile = self.mtb[b]
            smt = self.sm_t[lpost % 2]
            for kc in range(16):
                mk = self.bgf(mlo + kc * 1024, 256)
                self.stt("dve", mk, mk, self.gain(lpost, wpost, kc), self.rs[:, tt * 256:(tt + 1) * 256],
                         ALU.mult, ALU.mult, [mtile, smt, self.rs_t[hf]], [mtile])
            self.tt("dve", self.bgf(xlo, 4096), self.bgf(xlo, 4096), self.bgf(mlo, 4096), ALU.add,
                    [xtile, mtile], [xtile])

        if pre == "issue":
            issue_loads(0)
            residual(0)
            return
        if pre is None:
            issue_loads(0)
        for i, tt in enumerate(tts):
            b = i % 2
            hf = tt // 4
            col = (tt % 4) * 256
            xlo, xtile = self.xtb[b]
            mlo, mtile = self.mtb[b]
            slo, stile = self.sqb[b]
            if i + 1 < len(tts):
                issue_loads(i + 1)
            if mix is not None and not (i == 0 and pre == "done"):
                smt = self.sm_t[lpost % 2]
                for kc in range(16):
                    mk = self.bgf(mlo + kc * 1024, 256)
                    self.stt("dve", mk, mk, self.gain(lpost, wpost, kc), self.rs[:, tt * 256:(tt + 1) * 256],
                             ALU.mult, ALU.mult, [mtile, smt, self.rs_t[hf]], [mtile])
                self.tt("dve", self.bgf(xlo, 4096), self.bgf(xlo, 4096), self.bgf(mlo, 4096), ALU.add,
                        [xtile, mtile], [xtile])
            if dst is not None:
                xv = self.bgf(xlo, 4096).rearrange("p (c t) -> p c t", c=16)
                self.store(dst[tt], self.bgf(xlo, 4096), [xtile], [dst_tiles[tt]], final=final)
            if lnext is not None:
                smt = self.sm_t[lnext % 2]
                self.act(self.bgb(slo, 4096), self.bgf(xlo, 4096), AF.Square, [xtile], [stile])
                bk = 4 + b
                for kc in range(16):
                    self.mm(self.bank[bk][:, 0:256], self.ones, self.bgb(slo + kc * 512, 256),
                            kc == 0, kc == 15, [stile, self.cst_t], [self.bank_t[bk]])
                self.act(self.rsn[b][:], self.bank[bk][:, 0:256], AF.Sqrt, [self.bank_t[bk], self.cst_t],
                         [self.rsn_t[b]], bias=self.epsc[:, 0:1], scale=1.0 / D)
                self.recip(self.rsn[b][:], self.rsn[b][:], [self.rsn_t[b]], [self.rsn_t[b]])
                for kc in range(16):
                    self.stt("dve", self.ht(hf, kc)[:, col:col + 256], self.bgf(xlo + kc * 1024, 256),
                             self.gain(lnext, wnext, kc), self.rsn[b][:], ALU.mult, ALU.mult,
                             [xtile, smt, self.rsn_t[b]], [self.ht_t[hf][kc]])

    def rhs_ht(self, half):
        def f(kc, tg):
            return self.ht(half, kc)[:, tg * 512:(tg + 1) * 512], self.ht_t[half][kc]
        return f

    def defer(self, fn, delay=2):
        self.deferred.append([delay, fn])

    def tick(self):
        keep = []
        for it in self.deferred:
            it[0] -= 1
            if it[0] <= 0:
                it[1]()
            else:
                keep.append(it)
        self.deferred = keep

    def flush_deferred(self):
        d, self.deferred = self.deferred, []
        for it in d:
            it[1]()

    def bg_rate(self, x):
        self.bg_acc += x
        k = int(self.bg_acc)
        self.bg_acc -= k
        if k:
            self.bg_step(k)

    def bg_step(self, k):
        if self.bg is None:
            return
        ph = self.P.phase
        self.P.phase = "proj"
        for _ in range(k):
            try:
                next(self.bg)
            except StopIteration:
                self.bg = None
                self.flush_deferred()
                break
            self.tick()
        self.P.phase = ph

    def bg_drain(self):
        while self.bg is not None:
            self.bg_step(64)
        self.flush_deferred()

    def gen_proj(self, l, unit, dst, evac_engs):
        dlo, dtile = dst
        w = self.wload(self.w16[l * 196 + unit], 2048)
        slot, wt = w
        for half in range(2):
            b = self.next_slot()
            for kc in range(16):
                lhsT = slot[:, kc * 128:(kc + 1) * 128]
                for tg in range(2):
                    self.mm(self.bank[b[tg]][:, 0:512], lhsT,
                            self.ht(half, kc)[:, tg * 512:(tg + 1) * 512], kc == 0, kc == 15,
                            [wt, self.ht_t[half][kc]], [self.bank_t[b[tg]]])
                if kc % 2 == 1:
                    yield

            def evac(b=b, half=half):
                for tg in range(2):
                    o = self.bgb(dlo + (half * 1024 + tg * 512) * 2, 512)
                    self.copy(evac_engs[tg], o, self.bank[b[tg]][:, 0:512], [self.bank_t[b[tg]]], [dtile])
            self.defer(evac)

    def gen_head(self, l, head, s):
        self.P.phase = "proj"
        if head[0] == "A":
            _, g, hh = head
            units = [(sx * 3 + g) * 4 + hh for sx in range(3)]
            d = DIL[g]
        else:
            _, h = head
            units = [36 + sx * 8 + h for sx in range(3)]
            d = 1
        engs = ("act", "dve") if head[0] == "A" else ("dve", "dve")
        for unit, dst in zip(units, (self.qts[s], self.kts[s], self.vt)):
            for _ in self.gen_proj(l, unit, dst, engs):
                self.P.phase = "proj"
                yield
        self.flush_deferred()
        vtlo, vttile = self.vt
        vlo, vtile = self.vs[s]
        nb = 16 // d
        for grp in range(2):
            b = self.next_slot()
            bk = b[0]
            for j in range(8):
                t = grp * 8 + j
                r, n = t // nb, t % nb
                s0 = r + d * 128 * n
                src = self.BG[:, _ssl(vtlo // 2 + s0, 128, d)]
                self.tr(self.bankb[bk][:, j * 128:(j + 1) * 128], src, self.ident,
                        [vttile, self.cst_t], [self.bank_t[bk]])
                if j % 4 == 3:
                    yield

            def evac(bk=bk, grp=grp):
                self.copy("dve", self.bgb(vlo + grp * 2048, 1024), self.bankb[bk][:, 0:1024],
                          [self.bank_t[bk]], [vtile])
            self.defer(evac, 1)

    def dilated(self, l, g, hh, s):
        self.P.phase = "dil%d" % g
        d = DIL[g]
        nb = 16 // d
        a = g * 4 + hh
        bmi = (hh * 3 + g) % 2
        qlo, qtile = self.qts[s]
        klo, ktile = self.kts[s]
        vlo, vtile = self.vs[s]
        ulo, utile = self.uacc
        ltiles = list(self.rs_t)
        OB, LB = 6, 7
        items = [(r, n) for r in range(d) for n in range(nb)]

        def smm(idx):
            r, n = items[idx]
            s0 = r + d * 128 * n
            nq = 256 if n + 1 < nb else 128
            sbk = 4 + idx % 2
            kap = self.BG[:, _ssl(klo // 2 + s0, 128, d)]
            qap = self.BG[:, _ssl(qlo // 2 + s0, nq, d)]
            self.mm(self.bank[sbk][:, 0:nq], kap, qap, True, True, [ktile, qtile], [self.bank_t[sbk]])

        smm(0)
        for idx, (r, n) in enumerate(items):
            t = r * nb + n
            nq = 256 if n + 1 < nb else 128
            pb = idx % 2
            sbk = 4 + pb
            if idx + 1 < len(items):
                smm(idx + 1)
            self.stt("dve", self.ts[pb][:, 0:nq], self.bank[sbk][:, 0:nq], SCALE, self.bm[bmi][:, 0:nq],
                     ALU.mult, ALU.add, [self.bank_t[sbk], self.bm_t[bmi]], [self.ts_t[pb]])
            self.act(self.pt[pb][:, 0:nq], self.ts[pb][:, 0:nq], AF.Exp, [self.ts_t[pb]], [self.pt_t[pb]])
            if 1 <= idx <= 4:
                self.run_late()
            self.bg_rate(3.3)
            cb = (n % 4) * 128
            vcur = self.bgb(vlo + t * 256, 128)
            for (bk, cur, prev) in ((OB, vcur, None if n == 0 else self.bgb(vlo + (t - 1) * 256, 128)),
                                    (LB, self.ones, None if n == 0 else self.ones)):
                o = self.bank[bk][:, cb:cb + 128]
                rd = [vtile, self.cst_t]
                if prev is not None:
                    self.mm(o, prev, self.pt[1 - pb][:, 128:256], True, False,
                            rd + [self.pt_t[1 - pb]], [self.bank_t[bk]])
                    self.mm(o, cur, self.pt[pb][:, 0:128], False, True,
                            rd + [self.pt_t[pb]], [self.bank_t[bk]])
                else:
                    self.mm(o, cur, self.pt[pb][:, 0:128], True, True,
                            rd + [self.pt_t[pb]], [self.bank_t[bk]])
            if n % 4 == 3 or n == nb - 1:
                n0 = n - n % 4
                c = (n % 4 + 1) * 128
                st = r + d * 128 * n0
                for (bk, dstv, tiles) in ((OB, self.BGf[:, _ssl(ulo // 4 + st, c, d)], [utile]),
                                          (LB, self.rs[:, _ssl(st, c, d)], ltiles)):
                    if g == 0:
                        self.copy("dve", dstv, self.bank[bk][:, 0:c], [self.bank_t[bk]], tiles)
                    else:
                        self.tt("dve", dstv, self.bank[bk][:, 0:c], dstv, ALU.add,
                                [self.bank_t[bk]] + tiles, tiles)

    def finish_a(self, hh, k):
        ph = self.P.phase
        self.P.phase = "dilfin"
        ulo, utile = self.uacc
        lt = [self.rs_t[k // 2]]
        ylo, ytile = self.ya[hh]
        c = slice(k * 512, (k + 1) * 512)
        self.recip(self.rs[:, c], self.rs[:, c], lt, lt)
        self.tt("dve", self.bgb(ylo + k * 1024, 512), self.bgf(ulo + k * 2048, 512), self.rs[:, c], ALU.mult,
                [utile] + lt, [ytile])
        self.P.phase = ph

    def forget(self, l):
        self.P.phase = "forget"
        smt = self.sm_t[l % 2]
        sm = self.sm[l % 2]
        self.load("pool", self.wf[:], self.wfd[l], (), [self.wf_t])
        for t in range(16):
            hf, col = t // 8, (t % 8) * 128
            for kc in range(16):
                self.mm(self.bank[4][:, t * 8:(t + 1) * 8], self.ht(hf, kc)[:, col:col + 128],
                        self.wf[:, kc * 8:(kc + 1) * 8], kc == 0, kc == 15,
                        [self.ht_t[hf][kc], self.wf_t], [self.bank_t[4]])
        self.tt("dve", self.lz[:], self.bank[4][:, 0:128], sm[:, 416:544], ALU.add,
                [self.bank_t[4], smt], [self.lz_t])
        self.act(self.l1[:], self.lz[:], AF.Exp, [self.lz_t], [self.l1_t], scale=-1.0)
        self.act(self.l1[:], self.l1[:], AF.Ln, [self.l1_t, self.cst_t], [self.l1_t], bias=self.epsc[:, 1:2])
        for t in range(16):
            self.mm(self.bank[5][:, t * 8:(t + 1) * 8], self.negones, self.l1[:, t * 8:(t + 1) * 8],
                    True, True, [self.l1_t, self.cst_t], [self.bank_t[5]])
        self.copy("dve", self.tot[:], self.bank[5][:, 0:128], [self.bank_t[5]], [self.tot_t])
        self.memset("dve", self.pre[:, 0:8], 0.0, [self.pre_t])
        for t in range(1, 16):
            self.tt("dve", self.pre[:, t * 8:(t + 1) * 8], self.pre[:, (t - 1) * 8:t * 8],
                    self.tot[:, (t - 1) * 8:t * 8], ALU.add, [self.pre_t, self.tot_t], [self.pre_t])
        for t in range(16):
            self.mm(self.bank[6][:, t * 8:(t + 1) * 8], self.negtri, self.l1[:, t * 8:(t + 1) * 8],
                    True, True, [self.l1_t, self.cst_t], [self.bank_t[6]])
        for t in range(16):
            self.mm(self.bank[7][:, t * 8:(t + 1) * 8], self.neghalf, self.l1[:, t * 8:(t + 1) * 8],
                    True, True, [self.l1_t, self.cst_t], [self.bank_t[7]])
        self.tt("dve", self.call[:], self.bank[6][:, 0:128], self.pre[:], ALU.add,
                [self.bank_t[6], self.pre_t], [self.call_t])
        self.tt("dve", self.ref[:], self.bank[7][:, 0:128], self.pre[:], ALU.add,
                [self.bank_t[7], self.pre_t], [self.ref_t])

    def fox_fb(self, h):
        fb, fbt = self.fb[h % 2], self.fb_t[h % 2]
        for i in range(16):
            self.tsc("dve", fb[:, i * 16:(i + 1) * 16], self.call[:, h:128:8], -1.0,
                     self.ref[:, i * 8 + h:i * 8 + h + 1], ALU.mult, ALU.add,
                     [self.call_t, self.ref_t], [fbt])

    def fox(self, l, h, s):
        self.P.phase = "fox"
        fbi = h % 2
        fb, fbt = self.fb[fbi], self.fb_t[fbi]
        if h == 0:
            self.fox_fb(0)
        qlo, qtile = self.qts[s]
        klo, ktile = self.kts[s]
        vlo, vtile = self.vs[s]
        ylo, ytile = self.yb[h]
        OB, LB = 6, 7
        items = [(G, j) for G in range(4) for j in range(4 * G + 4)]

        def smm(idx):
            G, j = items[idx]
            c0 = max(j - 4 * G, 0) * 128
            sbk = 4 + idx % 2
            self.mm(self.bank[sbk][:, c0:512], self.bgb(klo + j * 256, 128),
                    self.bgb(qlo + (G * 512 + c0) * 2, 512 - c0), True, True,
                    [ktile, qtile], [self.bank_t[sbk]])

        smm(0)
        for idx, (G, j) in enumerate(items):
            last = 4 * G + 3
            a = max(j - 4 * G, 0)
            c0 = a * 128
            pb = idx % 2
            sbk = 4 + pb
            if idx + 1 < len(items):
                smm(idx + 1)
            for ib in range(a, 4):
                i = 4 * G + ib
                blk = self.pt[pb][:, ib * 128:(ib + 1) * 128]
                self.act(blk, self.bank[sbk][:, ib * 128:(ib + 1) * 128], AF.Exp,
                         [self.bank_t[sbk], fbt], [self.pt_t[pb]],
                         bias=fb[:, i * 16 + j:i * 16 + j + 1], scale=SCALE)
                if i == j:
                    self.tt("dve", blk, blk, self.trim, ALU.mult, [self.pt_t[pb], self.cst_t],
                            [self.pt_t[pb]])
            if 1 <= idx <= 4:
                self.run_late()
            if idx == 6 and h + 1 < 8:
                self.fox_fb(h + 1)
            self.bg_rate(4.0 if (j == 0 and G > 0) else 1.1)
            for (bk, lt) in ((OB, self.bgb(vlo + j * 256, 128)), (LB, self.ones)):
                self.mm(self.bank[bk][:, c0:512], lt, self.pt[pb][:, c0:512], j == 0, j == last,
                        [vtile, self.cst_t, self.pt_t[pb]], [self.bank_t[bk]])
            if j == last:
                self.act(self.rec[:], self.bank[LB][:, 0:512], AF.Ln, [self.bank_t[LB]], [self.rec_t])
                self.copy("act", self.osb, self.bank[OB][:, 0:512], [self.bank_t[OB]], [self.osb_t])
                self.act(self.rec[:], self.rec[:], AF.Exp, [self.rec_t], [self.rec_t], scale=-1.0)
                self.tt("dve", self.bgb(ylo + G * 1024, 512), self.osb, self.rec[:], ALU.mult,
                        [self.osb_t, self.rec_t], [ytile])

    def run_late(self, all_=False):
        while self.late:
            f = self.late.pop(0)
            f()
            if not all_:
                break

    def load_bm(self, hd):
        if hd[0] == "A":
            _, g, hh = hd
            bmi = (hh * 3 + g) % 2
            self.load("sp", self.bm[bmi][:], self.bmd[g * 4 + hh], (), [self.bm_t[bmi]])

    def heads(self, l):
        hs = [("A", g, hh) for hh in range(4) for g in range(3)] + [("B", h) for h in range(8)]
        self.late = []
        self.load_bm(hs[0])
        self.bg = self.gen_head(l, hs[0], 0)
        self.bg_drain()
        for i, hd in enumerate(hs):
            if i + 1 < len(hs):
                self.load_bm(hs[i + 1])
            self.bg = self.gen_head(l, hs[i + 1], (i + 1) % 2) if i + 1 < len(hs) else None
            if hd[0] == "A":
                self.dilated(l, hd[1], hd[2], i % 2)
                self.bg_drain()
                if hd[1] == 2:
                    self.late = [(lambda hh=hd[2], k=k: self.finish_a(hh, k)) for k in range(4)]
            else:
                self.fox(l, hd[1], i % 2)
                self.bg_drain()
        self.run_late(True)

    def merge(self, l):
        self.P.phase = "merge"
        for n in range(16):
            wga = self.wload(self.w16[l * 196 + 60 + n], 2048)
            wpa = self.wload(self.wpa[l * 16 + n], 512)
            wgb = self.wload(self.w16[l * 196 + 76 + n], 2048)
            wpb = self.wload(self.wpb[l * 16 + n], 1024)
            mlo, mstile = self.ms[n % 2]
            m1lo, m1tile = self.m1
            for half in range(2):
                def rhs_a(kc, tg, half=half):
                    lo, tl = self.ya[kc]
                    return self.bgb(lo + (half * 1024 + tg * 512) * 2, 512), tl

                def rhs_b(kc, tg, half=half):
                    lo, tl = self.yb[kc]
                    return self.bgb(lo + (half * 1024 + tg * 512) * 2, 512), tl

                s0lo, s0t = self.sg[0]
                s1lo, s1t = self.sg[1]
                b = self.proj_half([wga], 16, self.rhs_ht(half))
                for tg in range(2):
                    self.act(self.bgf(s0lo + tg * 2048, 512), self.bank[b[tg]][:, 0:512], AF.Sigmoid,
                             [self.bank_t[b[tg]]], [s0t])
                b = self.proj_half([wpa], 4, rhs_a)
                for tg in range(2):
                    self.tt("dve", self.bgf(m1lo + tg * 2048, 512), self.bank[b[tg]][:, 0:512],
                            self.bgf(s0lo + tg * 2048, 512), ALU.mult, [self.bank_t[b[tg]], s0t], [m1tile])
                b = self.proj_half([wgb], 16, self.rhs_ht(half))
                for tg in range(2):
                    self.act(self.bgf(s1lo + tg * 2048, 512), self.bank[b[tg]][:, 0:512], AF.Sigmoid,
                             [self.bank_t[b[tg]]], [s1t])
                b = self.proj_half([wpb], 8, rhs_b)
                for tg in range(2):
                    sv = self.bgf(s1lo + tg * 2048, 512)
                    self.tt("dve", sv, self.bank[b[tg]][:, 0:512], sv, ALU.mult,
                            [self.bank_t[b[tg]], s1t], [s1t])
                    self.tt("dve", self.bgb(mlo + (half * 1024 + tg * 512) * 2, 512),
                            self.bgf(m1lo + tg * 2048, 512), sv, ALU.add, [m1tile, s1t], [mstile])
            self.store(self.mrg[n], self.bgb(mlo, 2048), [mstile], [self.mrg_t])

    def evac_mix(self, b, mx, sq, drams, dram_tile, ssq_banks, first, last_):
        for tg in range(2):
            mxv, mxt = mx[tg]
            sqv, sqt = sq[tg]
            self.copy("act", mxv, self.bank[b[tg]][:, 0:512], [self.bank_t[b[tg]]], [mxt])
            self.act(sqv, self.bank[b[tg]][:, 0:512], AF.Square, [self.bank_t[b[tg]]], [sqt])
            self.store(drams[tg], mxv.rearrange("p (a t) -> p a t", a=2), [mxt], [dram_tile])

        def ssq(sq=sq, ssq_banks=ssq_banks, first=first, last_=last_):
            for tg in range(2):
                sqv, sqt = sq[tg]
                bk = ssq_banks[tg]
                self.mm(self.bank[bk][:, 0:512], self.ones, sqv, first, last_,
                        [sqt, self.cst_t], [self.bank_t[bk]])
        self.pending.append(ssq)

    def make_rs(self, bk, q):
        hf = q // 2
        o = self.rs[:, q * 512:(q + 1) * 512]
        self.act(o, self.bank[bk][:, 0:512], AF.Sqrt, [self.bank_t[bk], self.cst_t], [self.rs_t[hf]],
                 bias=self.epsc[:, 0:1], scale=1.0 / D)
        self.recip(o, o, [self.rs_t[hf]], [self.rs_t[hf]])

    def side_step(self):
        if self.side is None:
            return
        ph = self.P.phase
        self.P.phase = "norm"
        try:
            next(self.side)
        except StopIteration:
            self.side = None
        self.P.phase = ph

    def side_drain(self):
        while self.side is not None:
            self.side_step()

    def gen_norm(self, tts, src, src_tiles, mix, mix_tiles, lpost, wpost, dst, dst_tiles,
                 lnext, wnext, final, xlo, mlo, slo, bk, wt=(10, 1, 2, 10, 1), eng="dve"):
        P = self.P
        xtile = P.tile("gx%d" % xlo, "BG", xlo, xlo + 16384)
        mtile = P.tile("gm%d" % mlo, "BG", mlo, mlo + 16384)
        stile = P.tile("gs%d" % slo, "BG", slo, slo + 8192)
        xv3 = self.bgf(xlo, 4096).rearrange("p (c t) -> p c t", c=16)
        mv3 = self.bgf(mlo, 4096).rearrange("p (c t) -> p c t", c=16)
        def ld_x(tt):
            rd = [src_tiles[tt]] if src_tiles is not None else []
            self.load("sp", self.bgf(xlo, 4096), src[tt], rd, [xtile])

        def ld_m(tt):
            self.load("sp", self.bgf(mlo, 4096), mix[tt], [mix_tiles[tt // 4]], [mtile])

        ld_x(tts[0])
        ld_m(tts[0])
        for _ in range(6):
            yield
        for ti, tt in enumerate(tts):
            hf = tt // 4
            col = (tt % 4) * 256
            nxt_tt = tts[ti + 1] if ti + 1 < len(tts) else None
            for _ in range(wt[0]):
                yield
            smt = self.sm_t[lpost % 2]
            for k0 in (0, 8):
                for kc in range(k0, k0 + 8):
                    mk = self.bgf(mlo + kc * 1024, 256)
                    self.stt("dve", mk, mk, self.gain(lpost, wpost, kc), self.rs[:, tt * 256:(tt + 1) * 256],
                             ALU.mult, ALU.mult, [mtile, smt, self.rs_t[hf]], [mtile])
                yield
            for _ in range(wt[1]):
                yield
            self.tt(eng, self.bgf(xlo, 2048), self.bgf(xlo, 2048), self.bgf(mlo, 2048), ALU.add,
                    [xtile, mtile], [xtile])
            yield
            self.tt(eng, self.bgf(xlo + 8192, 2048), self.bgf(xlo + 8192, 2048), self.bgf(mlo + 8192, 2048),
                    ALU.add, [xtile, mtile], [xtile])
            self.store(dst[tt], self.bgf(xlo, 4096), [xtile], [dst_tiles[tt]], final=final)
            if nxt_tt is not None:
                ld_m(nxt_tt)
            yield
            if lnext is None:
                if nxt_tt is not None:
                    ld_x(nxt_tt)
                continue
            for _ in range(wt[2]):
                yield
            self.act(self.bgb(slo, 2048), self.bgf(xlo, 2048), AF.Square, [xtile], [stile])
            yield
            self.act(self.bgb(slo + 4096, 2048), self.bgf(xlo + 8192, 2048), AF.Square, [xtile], [stile])
            for _ in range(1 + wt[3]):
                yield
            for kc in range(16):
                self.mm(self.bank[bk][:, 0:256], self.ones, self.bgb(slo + kc * 512, 256),
                        kc == 0, kc == 15, [stile, self.cst_t], [self.bank_t[bk]])
            for _ in range(1 + wt[4]):
                yield
            smt = self.sm_t[lnext % 2]
            self.act(self.rsn[0][:], self.bank[bk][:, 0:256], AF.Sqrt, [self.bank_t[bk], self.cst_t],
                     [self.rsn_t[0]], bias=self.epsc[:, 0:1], scale=1.0 / D)
            self.recip(self.rsn[0][:], self.rsn[0][:], [self.rsn_t[0]], [self.rsn_t[0]])
            for k0 in (0, 8):
                for kc in range(k0, k0 + 8):
                    self.stt("dve", self.ht(hf, kc)[:, col:col + 256], self.bgf(xlo + kc * 1024, 256),
                             self.gain(lnext, wnext, kc), self.rsn[0][:], ALU.mult, ALU.mult,
                             [xtile, smt, self.rsn_t[0]], [self.ht_t[hf][kc]])
                yield
            if nxt_tt is not None:
                ld_x(nxt_tt)

    def wo_phase(self, l, side_for_half1):
        self.P.phase = "wo"
        for half in range(2):
            for kc in range(16):
                lo, tl = self.mg[half][kc]
                self.load("sp", self.bgb(lo, 1024), self.mrg[kc][:, half * 1024:(half + 1) * 1024],
                          [self.mrg_t], [tl])
        it = 0
        for half in range(2):
            for n in range(16):
                w = self.wload(self.w16[l * 196 + 92 + n], 2048)

                def rhs(kc, tg, half=half):
                    lo, tl = self.mg[half][kc]
                    return self.bgb(lo + tg * 1024, 512), tl
                b = self.proj_half([w], 16, rhs)
                p = it % 2
                it += 1
                mxlo, mxt = self.mx[p]
                sqlo, sqt = self.sqw[p]
                mx = [(self.bgf(mxlo + tg * 2048, 512), mxt) for tg in range(2)]
                sq = [(self.bgb(sqlo + tg * 1024, 512), sqt) for tg in range(2)]
                drams = [self.mixa[half * 4 + tg * 2: half * 4 + tg * 2 + 2, :, n * 256:(n + 1) * 256]
                         .rearrange("a p t -> p a t") for tg in range(2)]
                self.evac_mix(b, mx, sq, drams, self.mixa_t[half], (4 + half * 2, 5 + half * 2),
                              n == 0, n == 15)
                if half == 1:
                    for _ in range(2):
                        self.side_step()
            self.flush()
            for tg in range(2):
                self.make_rs(4 + half * 2 + tg, half * 2 + tg)
            if half == 0:
                self.side = side_for_half1()
        self.side_drain()

    def ffn_half(self, l, hf):
        self.P.phase = "ffn_up"
        smt = self.sm_t[l % 2]
        sm = self.sm[l % 2]
        fs = self.fs
        av, at = fs["A"]
        bv, btl = fs["B"]
        for c in range(NFC):
            wg = self.wload(self.w16[l * 196 + 108 + c], 2048)
            wv = self.wload(self.w16[l * 196 + 108 + NFC + c], 2048)
            bg_ = self.proj_half([wg], 16, self.rhs_ht(hf))
            bv_ = self.proj_half([wv], 16, self.rhs_ht(hf))
            alo, atile = self.actt[c]
            kinds = ((bg_, c, "ugg", av, at), (bv_, NFC + c, "ugv", bv, btl))
            for (b, ch, ugn, xv, xt) in kinds:
                ug, ugt = fs[ugn]
                for tg in range(2):
                    self.copy("act", ug[:, 2 + tg * 512:2 + (tg + 1) * 512], self.bank[b[tg]][:, 0:512],
                              [self.bank_t[b[tg]]], [ugt])
            for (b, ch, ugn, xv, xt) in kinds:
                ug, ugt = fs[ugn]
                if hf == 0:
                    self.memset("dve", ug[:, 0:2], 0.0, [ugt])
                else:
                    self.copy("dve", ug[:, 0:2], self.halo[:, ch * 2:ch * 2 + 2], [self.halo_t], [ugt])
                cw = lambda j, ch=ch: sm[:, 64 + ch * 3 + j:64 + ch * 3 + j + 1]
                cb = sm[:, 328 + ch:328 + ch + 1]
                self.act(xv, ug[:, 2:1026], AF.Identity, [ugt, smt], [xt], bias=cb, scale=cw(2))
                if hf == 0:
                    self.copy("dve", self.halo[:, ch * 2:ch * 2 + 2], ug[:, 1024:1026], [ugt], [self.halo_t])
                self.stt("dve", xv, ug[:, 1:1025], cw(1), xv, ALU.mult, ALU.add, [ugt, smt, xt], [xt])
                self.stt("dve", xv, ug[:, 0:1024], cw(0), xv, ALU.mult, ALU.add, [ugt, smt, xt], [xt])
            self.act(av, av, AF.Gelu_apprx_tanh, [at], [at])
            self.tt("dve", self.bgb(alo, 1024), av, bv, ALU.mult, [at, btl], [atile])
            if c == 20:
                self.side_drain()
            for _ in range(7):
                self.side_step()
        self.side_drain()
        self.P.phase = "ffn_down"
        it = 0
        for n in range(16):
            ws = [self.wload(self.wdn[l * 16 + n][:, 0:2048], 2048),
                  self.wload(self.wdn[l * 16 + n][:, 2048:4096], 2048),
                  self.wload(self.wdn[l * 16 + n][:, 4096:5632], 1536)]

            def rhs(kc, tg):
                lo, tl = self.actt[kc]
                return self.bgb(lo + tg * 1024, 512), tl
            b = self.proj_half(ws, NFC, rhs)
            mx = [fs["mx%d" % tg] for tg in range(2)]
            sq = [fs["sq%d" % tg] for tg in range(2)]
            drams = [self.mixf[hf * 4 + tg * 2: hf * 4 + tg * 2 + 2, :, n * 256:(n + 1) * 256]
                     .rearrange("a p t -> p a t") for tg in range(2)]
            self.evac_mix(b, mx, sq, drams, self.mixf_t[hf], (4, 5), n == 0, n == 15)
        self.flush()
        for tg in range(2):
            self.make_rs(4 + tg, hf * 2 + tg)

    def build(self):
        L = self.L
        with ExitStack() as es:
            self.alloc(es)
            self.setup()
            self.load_smalls(0)
            self.norm_pass(list(range(8)), self.xT, None, None, None, None, None, None, None, 0, 0)
            for l in range(L):
                last = l == L - 1
                if not last:
                    self.load_smalls(l + 1)
                self.forget(l)
                self.heads(l)
                self.merge(l)
                src = self.xT if l == 0 else self.xs
                src_t = None if l == 0 else self.xs_t
                nxt = None if last else l + 1
                fdst = self.outT if last else self.xs
                fdst_t = self.out_t if last else self.xs_t

                def n0(l=l, src=src, src_t=src_t):
                    self.norm_pass([0, 1, 2, 3], src, src_t, self.mixa, self.mixa_t, l, 1,
                                   self.xs, self.xs_t, l, 2, pre="issue")
                    self.P.phase = "wo"
                    return None
                self.wo_phase(l, n0)
                self.norm_pass([0, 1, 2, 3], src, src_t, self.mixa, self.mixa_t, l, 1, self.xs, self.xs_t, l, 2,
                               pre="done")
                self.side = self.gen_norm([4, 5, 6, 7], src, src_t, self.mixa, self.mixa_t, l, 1,
                                          self.xs, self.xs_t, l, 2, False, 49152, 65536, 81920, 6)
                self.ffn_half(l, 0)
                self.side = self.gen_norm([0, 1, 2, 3], self.xs, self.xs_t, self.mixf, self.mixf_t, l, 3,
                                          fdst, fdst_t, nxt, 0, last, 49152, 65536, 81920, 6)
                self.ffn_half(l, 1)
                self.norm_pass([4, 5, 6, 7], self.xs, self.xs_t, self.mixf, self.mixf_t, l, 3,
                               fdst, fdst_t, nxt, 0, final=last)
            self.P.emit(self.nc)
        return self.nc


def _units(W):
    K, N = W.shape
    return np.ascontiguousarray(
        W.reshape(K // 128, 128, N // 128, 128).transpose(2, 1, 0, 3).reshape(N // 128, 128, K))


def _bucket(dist):
    dist = np.asarray(dist, dtype=np.int64)
    nf = np.maximum(dist, 1).astype(np.float32)
    large = 16 + (np.log(nf / np.float32(16)) / np.float32(math.log(2048 / 16))
                  * np.float32(16)).astype(np.int32)
    large = np.minimum(large, 31)
    return np.where(dist < 16, dist, large)


def _consts():
    k = np.arange(128)[:, None]
    q = np.arange(128)[None, :]
    negtri = -(k <= q).astype(np.float32)
    negones = -np.ones((128, 128), np.float32)
    neghalf = -np.broadcast_to((k < 64), (128, 128)).astype(np.float32)
    cst = np.concatenate([negtri, negones, neghalf], axis=1)
    ident = np.eye(128, dtype=np.float32)
    ones = np.ones((128, 128), np.float32)
    trim = (k <= q).astype(np.float32)
    cstb = np.concatenate([ident, ones, trim], axis=1)
    return np.ascontiguousarray(cst), np.ascontiguousarray(cstb)


def _bias_mats(rel_bias):
    j = np.arange(128)[:, None]
    i = np.arange(128)[None, :]
    out = np.full((12, 128, 256), NEGM, np.float32)
    for g in range(3):
        d = DIL[g]
        dc = i - j
        dn = 128 + i - j
        bc = _bucket(np.maximum(dc, 0) * d)
        bn = _bucket(np.clip(dn, 0, 128) * d)
        for hh in range(4):
            a = g * 4 + hh
            col = rel_bias[:, a]
            out[a, :, 0:128] = np.where(dc >= 0, col[bc], np.float32(NEGM))
            out[a, :, 128:256] = np.where(dn <= 128, col[bn], np.float32(NEGM))
    return out


def _layer_inputs(l, w_in, b_f, w_pa, w_pb, w_o, w_up, conv_w, conv_b, w_down,
                  g_mix_pre, g_mix_post, g_ffn_pre, g_ffn_post):
    wi = w_in[l]
    w16 = np.concatenate([_units(wi[:, :7680]), _units(wi[:, 7688:]), _units(w_o[l]), _units(w_up[l])], axis=0)
    wf = np.ascontiguousarray(wi[:, 7680:7688].reshape(16, 128, 8).transpose(1, 0, 2).reshape(128, 128))
    sm = np.empty((128, SMW), np.float32)
    for i, gsrc in enumerate((g_mix_pre, g_mix_post, g_ffn_pre, g_ffn_post)):
        sm[:, i * 16:(i + 1) * 16] = gsrc[l].reshape(16, 128).T
    sm[:, 64:328] = conv_w[l].reshape(3, 88, 128).transpose(2, 1, 0).reshape(128, 264)
    sm[:, 328:416] = conv_b[l].reshape(88, 128).T
    sm[:, 416:544] = np.tile(b_f[l][None, :], (128, 16))
    return dict(w16=w16, wpa=_units(w_pa[l]), wpb=_units(w_pb[l]), wdn=_units(w_down[l]), wf=wf, sm=sm)


_PROG_CACHE = {}


def _get_prog(L):
    if L not in _PROG_CACHE:
        _PROG_CACHE[L] = Builder(L).build()
    return _PROG_CACHE[L]


def _run(xT_list, layers, rel_bias, weights, core_ids):
    L = len(layers)
    per = [_layer_inputs(l, *weights) for l in layers]
    shared = {k: np.ascontiguousarray(np.concatenate([p[k] for p in per], axis=0)) if k in ("w16", "wpa", "wpb", "wdn")
              else np.ascontiguousarray(np.stack([p[k] for p in per], axis=0)) for k in per[0]}
    cst, cstb = _consts()
    shared["bm"] = _bias_mats(rel_bias)
    shared["cst"] = cst
    shared["cstb"] = cstb
    nc = _get_prog(L)
    in_maps = [dict(shared, xT=xt) for xt in xT_list]
    res = run_bass_kernel_spmd(nc, in_maps, core_ids=core_ids)
    return [r["outT"] for r in res.results]


def _x_to_dev(xb):
    return np.ascontiguousarray(xb.reshape(8, 256, NKC, 128).transpose(0, 3, 2, 1)).reshape(8, 128, 4096)


def _dev_to_x(o):
    return np.asarray(o).reshape(8, 128, NKC, 256).transpose(0, 3, 2, 1).reshape(T, D)


FUSE = 4


def kernel(x, rel_bias, w_in, b_f, w_pa, w_pb, w_o, w_up, conv_w, conv_b, w_down,
           g_mix_pre, g_mix_post, g_ffn_pre, g_ffn_post):
    f = lambda a: np.asarray(a, dtype=np.float32)
    x = f(x)
    weights = tuple(f(a) for a in (w_in, b_f, w_pa, w_pb, w_o, w_up, conv_w, conv_b, w_down,
                                   g_mix_pre, g_mix_post, g_ffn_pre, g_ffn_post))
    rel_bias = f(rel_bias)
    B = x.shape[0]
    xT = [_x_to_dev(x[b]) for b in range(B)]
    for l0 in range(0, DEPTH, FUSE):
        xT = _run(xT, list(range(l0, l0 + FUSE)), rel_bias, weights, list(range(B)))
    out = np.stack([_dev_to_x(o) for o in xT], axis=0)
    return np.ascontiguousarray(out.astype(np.float32))
```
